# Optimizing a Trainium2 kernel written in Bass

```python
import math
import jax, jax.numpy as jnp
from jax import lax
import numpy as np

D_MODEL = 1024
BATCH = 8
SEQ = 4096
DEPTH = 1
DEC_BATCH = 16
DEC_SEQ = 4096
PAST_LEN = 128

MIX_WIDTH = D_MODEL
LRU_WIDTH = MIX_WIDTH // 2
POOL_WIDTH = MIX_WIDTH - LRU_WIDTH
LRU_HEADS = 8
LRU_HEAD_DIM = LRU_WIDTH // LRU_HEADS
CONV_WIDTH = 4
LRU_C = 8.0
POOL_WINDOWS = (2, 4, 8, 16)
POOL_GROUPS = len(POOL_WINDOWS)
POOL_GROUP_DIM = POOL_WIDTH // POOL_GROUPS
IN_WIDTH = 2 * LRU_WIDTH + POOL_WIDTH
D_FF = 2816
FFN_RES = 0.5
EPS = 1e-6

kernel_name = "hybrid_rglru_pool_macaron_encoder"


def rmsnorm(x, g):
    xf = x.astype(jnp.float32)
    ms = jnp.mean(xf * xf, axis=-1, keepdims=True)
    return (xf * lax.rsqrt(ms + EPS)).astype(x.dtype) * g


def swiglu(h, w_in, w_out):
    gu = h @ w_in
    g, u = jnp.split(gu, 2, axis=-1)
    return (jax.nn.silu(g) * u) @ w_out


def macaron_ffn(x, pre_g, post_g, w_in, w_out):
    y = swiglu(rmsnorm(x, pre_g), w_in, w_out)
    return x + FFN_RES * rmsnorm(y, post_g)


def centred_depthwise_conv(x, w, b):
    S = x.shape[1]
    left = CONV_WIDTH // 2
    xp = jnp.pad(x, ((0, 0), (left, CONV_WIDTH - 1 - left), (0, 0)))
    out = b.astype(jnp.float32)
    for k in range(CONV_WIDTH):
        out = out + xp[:, k:k + S, :] * w[k].astype(jnp.float32)
    return out


def _lin_combine(e1, e2):
    a1, b1 = e1
    a2, b2 = e2
    return a1 * a2, a2 * b1 + b2


def rglru_direction(xc, w_a, b_a, w_x, b_x, lam, reverse):
    B, S, R = xc.shape
    xh = xc.reshape(B, S, LRU_HEADS, LRU_HEAD_DIM)
    r = jax.nn.sigmoid(jnp.einsum('bshi,hij->bshj', xh, w_a.astype(jnp.float32)).reshape(B, S, R)
                       + b_a.astype(jnp.float32))
    i = jax.nn.sigmoid(jnp.einsum('bshi,hij->bshj', xh, w_x.astype(jnp.float32)).reshape(B, S, R)
                       + b_x.astype(jnp.float32))
    log_a = -LRU_C * r * jax.nn.softplus(-lam.astype(jnp.float32))
    a = jnp.exp(log_a)
    u = jnp.sqrt(-jnp.expm1(2.0 * log_a)) * (i * xc)
    _, h = lax.associative_scan(_lin_combine, (a, u), reverse=reverse, axis=1)
    return h


def pool_mixer(p, w_pool, scale):
    B, S, _ = p.shape
    pf = p.astype(jnp.float32)
    cs = jnp.concatenate([jnp.zeros((B, 1, POOL_WIDTH), jnp.float32), jnp.cumsum(pf, axis=1)], axis=1)
    t = jnp.arange(S)
    outs = []
    for g, w in enumerate(POOL_WINDOWS):
        sl = slice(g * POOL_GROUP_DIM, (g + 1) * POOL_GROUP_DIM)
        lo = jnp.clip(t - w // 2, 0, S)
        hi = jnp.clip(t + w // 2, 0, S)
        csg = cs[..., sl]
        cnt = (hi - lo).astype(jnp.float32)[None, :, None]
        mean = (jnp.take(csg, hi, axis=1) - jnp.take(csg, lo, axis=1)) / cnt
        d = (mean - pf[..., sl]).astype(p.dtype)
        outs.append(d @ w_pool[g])
    return jnp.concatenate(outs, axis=-1) * scale


def token_mixing(x, pre_g, post_g, w_in, conv_w, conv_b, lru_w_a, lru_b_a, lru_w_x, lru_b_x,
                 lru_lam, lru_out_g, pool_w, pool_scale, pool_out_g, w_out):
    h = rmsnorm(x, pre_g)
    z = h @ w_in
    xb = z[..., :LRU_WIDTH]
    gb = z[..., LRU_WIDTH:2 * LRU_WIDTH]
    pb = z[..., 2 * LRU_WIDTH:]
    xc = centred_depthwise_conv(xb.astype(jnp.float32), conv_w, conv_b)
    h_f = rglru_direction(xc, lru_w_a[0], lru_b_a[0], lru_w_x[0], lru_b_x[0], lru_lam[0], False)
    h_b = rglru_direction(xc, lru_w_a[1], lru_b_a[1], lru_w_x[1], lru_b_x[1], lru_lam[1], True)
    lru = (h_f + h_b).astype(x.dtype) * jax.nn.gelu(gb)
    lru = rmsnorm(lru, lru_out_g)
    pool = rmsnorm(pool_mixer(pb, pool_w, pool_scale), pool_out_g)
    o = jnp.concatenate([lru, pool], axis=-1) @ w_out
    return x + rmsnorm(o, post_g)


def encoder_layer(x, l, ffn1_pre_g, ffn1_post_g, ffn1_w_in, ffn1_w_out,
                  mix_pre_g, mix_post_g, w_in, conv_w, conv_b, lru_w_a, lru_b_a, lru_w_x, lru_b_x,
                  lru_lam, lru_out_g, pool_w, pool_scale, pool_out_g, w_out,
                  ffn2_pre_g, ffn2_post_g, ffn2_w_in, ffn2_w_out):
    x = macaron_ffn(x, ffn1_pre_g[l], ffn1_post_g[l], ffn1_w_in[l], ffn1_w_out[l])
    x = token_mixing(x, mix_pre_g[l], mix_post_g[l], w_in[l], conv_w[l], conv_b[l],
                     lru_w_a[l], lru_b_a[l], lru_w_x[l], lru_b_x[l], lru_lam[l], lru_out_g[l],
                     pool_w[l], pool_scale[l], pool_out_g[l], w_out[l])
    x = macaron_ffn(x, ffn2_pre_g[l], ffn2_post_g[l], ffn2_w_in[l], ffn2_w_out[l])
    return x


def setup_inputs(seed: int = 0) -> dict:
    key = jax.random.key(seed)
    ks = jax.random.split(key, 32)
    f32 = jnp.float32
    nrm = lambda k, shape, fan_in: jax.random.normal(k, shape, f32) * (fan_in ** -0.5)
    gain = lambda k, n: 1.0 + 0.05 * jax.random.normal(k, (DEPTH, n), f32)
    a_c = jax.random.uniform(ks[12], (DEPTH, 2, LRU_WIDTH), f32, 0.9, 0.999)
    p0 = a_c ** (1.0 / LRU_C)
    lam = jnp.log(p0) - jnp.log1p(-p0)
    return {
        "x_prompt": jax.random.normal(ks[0], (BATCH, SEQ, D_MODEL), f32),
        "x_sample": jax.random.normal(ks[1], (DEC_BATCH, DEC_SEQ, D_MODEL), f32),
        "ffn1_pre_g": gain(ks[2], D_MODEL),
        "ffn1_post_g": gain(ks[3], D_MODEL),
        "ffn1_w_in": nrm(ks[4], (DEPTH, D_MODEL, 2 * D_FF), D_MODEL),
        "ffn1_w_out": nrm(ks[5], (DEPTH, D_FF, D_MODEL), D_FF),
        "mix_pre_g": gain(ks[6], D_MODEL),
        "mix_post_g": gain(ks[7], D_MODEL),
        "w_in": nrm(ks[8], (DEPTH, D_MODEL, IN_WIDTH), D_MODEL),
        "conv_w": nrm(ks[9], (DEPTH, CONV_WIDTH, LRU_WIDTH), CONV_WIDTH),
        "conv_b": 0.02 * jax.random.normal(ks[10], (DEPTH, LRU_WIDTH), f32),
        "lru_w_a": nrm(ks[11], (DEPTH, 2, LRU_HEADS, LRU_HEAD_DIM, LRU_HEAD_DIM), LRU_HEAD_DIM),
        "lru_b_a": 0.02 * jax.random.normal(ks[13], (DEPTH, 2, LRU_WIDTH), f32),
        "lru_w_x": nrm(ks[14], (DEPTH, 2, LRU_HEADS, LRU_HEAD_DIM, LRU_HEAD_DIM), LRU_HEAD_DIM),
        "lru_b_x": 0.02 * jax.random.normal(ks[15], (DEPTH, 2, LRU_WIDTH), f32),
        "lru_lam": lam,
        "lru_out_g": gain(ks[16], LRU_WIDTH),
        "pool_w": nrm(ks[17], (DEPTH, POOL_GROUPS, POOL_GROUP_DIM, POOL_GROUP_DIM), POOL_GROUP_DIM),
        "pool_scale": 1.0 + 0.1 * jax.random.normal(ks[18], (DEPTH, POOL_WIDTH), f32),
        "pool_out_g": gain(ks[19], POOL_WIDTH),
        "w_out": nrm(ks[20], (DEPTH, MIX_WIDTH, D_MODEL), MIX_WIDTH),
        "ffn2_pre_g": gain(ks[21], D_MODEL),
        "ffn2_post_g": gain(ks[22], D_MODEL),
        "ffn2_w_in": nrm(ks[23], (DEPTH, D_MODEL, 2 * D_FF), D_MODEL),
        "ffn2_w_out": nrm(ks[24], (DEPTH, D_FF, D_MODEL), D_FF),
    }


def reference(x_prompt, x_sample, ffn1_pre_g, ffn1_post_g, ffn1_w_in, ffn1_w_out,
              mix_pre_g, mix_post_g, w_in, conv_w, conv_b, lru_w_a, lru_b_a, lru_w_x, lru_b_x,
              lru_lam, lru_out_g, pool_w, pool_scale, pool_out_g, w_out,
              ffn2_pre_g, ffn2_post_g, ffn2_w_in, ffn2_w_out):
    y_prompt = x_prompt
    y_sample = x_sample
    for l in range(DEPTH):
        y_prompt = encoder_layer(y_prompt, l, ffn1_pre_g, ffn1_post_g, ffn1_w_in, ffn1_w_out,
                                 mix_pre_g, mix_post_g, w_in, conv_w, conv_b, lru_w_a, lru_b_a,
                                 lru_w_x, lru_b_x, lru_lam, lru_out_g, pool_w, pool_scale,
                                 pool_out_g, w_out, ffn2_pre_g, ffn2_post_g, ffn2_w_in, ffn2_w_out)
        y_sample = encoder_layer(y_sample, l, ffn1_pre_g, ffn1_post_g, ffn1_w_in, ffn1_w_out,
                                 mix_pre_g, mix_post_g, w_in, conv_w, conv_b, lru_w_a, lru_b_a,
                                 lru_w_x, lru_b_x, lru_lam, lru_out_g, pool_w, pool_scale,
                                 pool_out_g, w_out, ffn2_pre_g, ffn2_post_g, ffn2_w_in, ffn2_w_out)
    return (y_prompt, y_sample)
```

```python
import numpy as np
from contextlib import ExitStack
import concourse.bass as bass
import concourse.mybir as mybir
from concourse.bass_utils import run_bass_kernel_spmd

F32 = mybir.dt.float32
BF16 = mybir.dt.bfloat16
AF = mybir.ActivationFunctionType
ALU = mybir.AluOpType
AX = mybir.AxisListType

D = 1024
DFF = 2816
NF = DFF // 128
NKD = D // 128
TT = 512
EPS = 1e-6
NPV = 96
WINS = (2, 4, 8, 16)
ARENA_WORDS = 53200
SCHED_WINDOW = 700
SCHED_FFN = True
TBL_BIAS = 1.0
BAN_POOL = False
ALT_ENGINES = ("DVE", "POOL")


class Chan:
    __slots__ = ("sem", "count")

    def __init__(self, sem):
        self.sem = sem
        self.count = 0


class Eng:
    def __init__(self, e, ch):
        self.e = e
        self.ch = ch
        self.seen = {}


class Res:
    __slots__ = ("lw", "rd")

    def __init__(self):
        self.lw = None
        self.rd = {}


def RL(n):
    return [Res() for _ in range(n)]


class KB:
    def __init__(self, nc, es):
        self.nc = nc
        self.es = es
        self.chans = []
        self.rec = None
        self.PE = Eng(nc.tensor, self.chan("c_pe"))
        self.ACT = Eng(nc.scalar, self.chan("c_act"))
        self.DVE = Eng(nc.vector, self.chan("c_dve"))
        self.POOL = Eng(nc.gpsimd, self.chan("c_pool"))
        self.SP = Eng(nc.sync, self.chan("c_sp"))
        self.engs = [self.PE, self.ACT, self.DVE, self.POOL, self.SP]
        self.alt_engines = [e for e, nm in ((self.DVE, "DVE"), (self.POOL, "POOL"), (self.ACT, "ACT")) if nm in ALT_ENGINES]

    def chan(self, name):
        c = Chan(self.es.enter_context(self.nc.semaphore(name)))
        self.chans.append(c)
        return c

    def _sync(self, eng, reads, writes):
        deps = {}
        for r in reads:
            if r.lw is not None:
                ch, c = r.lw
                if deps.get(ch, 0) < c:
                    deps[ch] = c
        for w in writes:
            if w.lw is not None:
                ch, c = w.lw
                if deps.get(ch, 0) < c:
                    deps[ch] = c
            for ch, c in w.rd.items():
                if deps.get(ch, 0) < c:
                    deps[ch] = c
        for ch, c in deps.items():
            if eng.seen.get(ch, 0) < c:
                eng.e.wait_ge(ch.sem, c)
                eng.seen[ch] = c

    def op(self, eng, fn, reads=(), writes=(), dur=None, n=512, cls=None, tbl=None, alts=None):
        if self.rec is not None:
            if dur is None:
                dur = self.est(eng, n, cls)
            al = [(eng, fn, dur)]
            for (e2, f2) in (alts or ()):
                if e2 in self.alt_engines:
                    al.append((e2, f2, self.est(e2, n, cls)))
            if BAN_POOL and eng is self.POOL and len(al) > 1:
                al = al[1:]
                eng, fn, dur = al[0]
            self.rec.append(dict(kind="op", eng=eng, fn=fn, reads=list(reads), writes=list(writes), dur=dur,
                                 occ=dur, tbl=tbl, alts=al))
            return
        self._sync(eng, reads, writes)
        ins = fn()
        ch = eng.ch
        ch.count += 1
        ins.then_inc(ch.sem, 1)
        c = ch.count
        for r in reads:
            r.rd[ch] = c
        for w in writes:
            w.lw = (ch, c)
            w.rd = {}

    def dma(self, qeng, chan, out, in_, reads=(), writes=(), nbytes=524288, **kw):
        if self.rec is not None:
            self.rec.append(dict(kind="dma", eng=qeng, chan=chan, out=out, in_=in_, reads=list(reads),
                                 writes=list(writes), kw=kw, dur=2.5 + nbytes / 150e3, occ=0.15, tbl=None))
            return
        self._sync(qeng, reads, writes)
        ins = qeng.e.dma_start(out=out, in_=in_, **kw)
        chan.count += 16
        ins.then_inc(chan.sem, 16)
        c = chan.count
        for r in reads:
            r.rd[chan] = c
        for w in writes:
            w.lw = (chan, c)
            w.rd = {}

    def est(self, eng, n, cls):
        if eng is self.ACT:
            return 0.30 + n / 1200.0
        if eng is self.DVE:
            if cls == "scan":
                return 0.2 + 2.8 * n / 960.0
            if cls == "fast":
                return 0.16 + n / 1920.0
            return 0.2 + 1.5 * n / 960.0
        if eng is self.POOL:
            return 0.3 + 2.6 * n / 1000.0
        if eng is self.PE:
            return 0.22
        return 0.2

    def begin_record(self):
        self.rec = []

    def flush(self, window=700, verbose=False):
        rec = self.rec
        self.rec = None
        n = len(rec)
        if n == 0:
            return
        lastw = {}
        readers = {}
        preds = [set() for _ in range(n)]
        for i, o in enumerate(rec):
            for r in o["reads"]:
                k = id(r)
                if k in lastw:
                    preds[i].add(lastw[k])
            for w in o["writes"]:
                k = id(w)
                if k in lastw:
                    preds[i].add(lastw[k])
                for j in readers.get(k, ()):
                    preds[i].add(j)
            for r in o["reads"]:
                readers.setdefault(id(r), []).append(i)
            for w in o["writes"]:
                k = id(w)
                lastw[k] = i
                readers[k] = []
            preds[i].discard(i)
        succs = [[] for _ in range(n)]
        for i in range(n):
            for j in preds[i]:
                succs[j].append(i)
        prio = [0.0] * n
        for i in range(n - 1, -1, -1):
            m = 0.0
            for j in succs[i]:
                if prio[j] > m:
                    m = prio[j]
            prio[i] = m + rec[i]["dur"]
        npred = [len(p) for p in preds]
        finish = [0.0] * n
        free = {}
        cand = set(i for i in range(n) if npred[i] == 0)
        done = [False] * n
        lo = 0
        order = []
        act_tbl = [None]
        LAT = 0.15
        while len(order) < n:
            while lo < n and done[lo]:
                lo += 1
            best = None
            bkey = None
            for i in cand:
                if i >= lo + window:
                    continue
                o = rec[i]
                rdy = 0.0
                for j in preds[i]:
                    f = finish[j] + LAT
                    if f > rdy:
                        rdy = f
                if o["kind"] == "dma":
                    e = o["eng"]
                    st = max(free.get(e, 0.0), rdy)
                    key = (st, -prio[i], i)
                    if bkey is None or key < bkey:
                        bkey = key
                        best = (i, st, 0.0, None)
                    continue
                bfin = None
                for ai, (e, f_, du) in enumerate(o["alts"]):
                    st = max(free.get(e, 0.0), rdy)
                    pen = 0.0
                    if o["tbl"] is not None and e is self.ACT and act_tbl[0] != o["tbl"]:
                        pen = 1.3
                    fin = st + pen + du
                    if bfin is None or fin < bfin[0]:
                        bfin = (fin, st, pen, ai)
                key = (bfin[1] + bfin[2] * TBL_BIAS, -prio[i], i)
                if bkey is None or key < bkey:
                    bkey = key
                    best = (i, bfin[1], bfin[2], bfin[3])
            i, st, pen, ai = best
            o = rec[i]
            if ai is not None:
                e, f_, du = o["alts"][ai]
                o["eng"], o["fn"], o["dur"], o["occ"] = e, f_, du, du
            e = o["eng"]
            if o["tbl"] is not None and e is self.ACT:
                if act_tbl[0] != o["tbl"]:
                    self.nswitch = getattr(self, "nswitch", 0) + 1
                act_tbl[0] = o["tbl"]
            st += pen
            free[e] = st + o["occ"]
            finish[i] = st + o["dur"]
            done[i] = True
            cand.discard(i)
            order.append(i)
            for j in succs[i]:
                npred[j] -= 1
                if npred[j] == 0:
                    cand.add(j)
        self.last_makespan = max(finish)
        if verbose:
            busy = {}
            for o in rec:
                busy[o["eng"]] = busy.get(o["eng"], 0.0) + o["occ"]
            print("[sched] ops=%d makespan=%.1fus critpath=%.1fus tblsw=%d busy=%s" % (
                n, self.last_makespan, max(prio), getattr(self, "nswitch", 0), {("PE", "ACT", "DVE", "POOL", "SP")[self.engs.index(e)]: round(b, 1)
                                        for e, b in busy.items()}))
        for i in order:
            o = rec[i]
            if o["kind"] == "op":
                self.op(o["eng"], o["fn"], o["reads"], o["writes"])
            else:
                self.dma(o["eng"], o["chan"], o["out"], o["in_"], o["reads"], o["writes"], **o["kw"])

    def barrier(self):
        for eng in self.engs:
            for ch in self.chans:
                if ch.count > eng.seen.get(ch, 0):
                    eng.e.wait_ge(ch.sem, ch.count)
                    eng.seen[ch] = ch.count


class Arena:
    def __init__(self, ap, nwords):
        self.ap = ap
        self.n = nwords
        self.off = 0

    def view(self, off, shape, dtype):
        n = int(np.prod(shape))
        words = n if dtype == F32 else (n + 1) // 2
        assert off + words <= self.n, (off, words, self.n)
        v = self.ap[:, off:off + words]
        if dtype != F32:
            v = v.bitcast(dtype)
        if len(shape) == 2:
            v = v.rearrange("p (a b) -> p a b", a=shape[0])
        return v, words

    def alloc(self, shape, dtype=F32):
        v, words = self.view(self.off, shape, dtype)
        self.off += words
        return v


def build(nseq, S):
    NT = S // TT
    ntok = nseq * S
    nc = bass.Bass("TRN2", target_bir_lowering=False)

    def din(name, shape):
        return nc.dram_tensor(name, shape, F32, kind="ExternalInput").ap()

    x_d = din("x", [ntok, D])
    y_d = nc.dram_tensor("y", [ntok, D], F32, kind="ExternalOutput").ap()
    s1_d = nc.dram_tensor("s1", [ntok, D], F32, kind="Internal").ap()
    s2_d = nc.dram_tensor("s2", [ntok, D], F32, kind="Internal").ap()
    ht_d = nc.dram_tensor("ht", [NKD, 128, ntok], BF16, kind="Internal").ap()
    hb_d = nc.dram_tensor("hb", [4, 128, ntok], F32, kind="Internal").ap()
    xc_d = nc.dram_tensor("xc", [4, 128, ntok], F32, kind="Internal").ap()
    w1a_d = din("w1a", [D, 2 * DFF])
    w1b_d = din("w1b", [DFF, D])
    w2a_d = din("w2a", [D, 2 * DFF])
    w2b_d = din("w2b", [DFF, D])
    win_d = din("win", [D, 1536])
    wout_d = din("wout", [D, D])
    gw_d = din("gw", [128, 16 * 128])
    pw_d = din("pw", [128, 4 * 128])
    pvec_d = din("pvec", [128, NPV])
    gbc_d = din("gbc", [128, 3 * D])
    ident_d = din("ident", [128, 128])

    with ExitStack() as es:
        arena_t = es.enter_context(nc.sbuf_tensor("arena", [128, ARENA_WORDS], F32))
        banks = [es.enter_context(nc.psum_tensor("bank%d" % i, [128, 512], F32)) for i in range(8)]
        K = KB(nc, es)
        PE, ACT, DVE, POOL, SP = K.PE, K.ACT, K.DVE, K.POOL, K.SP
        ar = Arena(arena_t[:, :], ARENA_WORDS)
        P = [b[:, :] for b in banks]
        rP = RL(8)

        PV = ar.alloc((NPV,), F32)
        IDB = ar.alloc((128,), BF16)
        ONEB = ar.alloc((2,), BF16)
        persist_off = ar.off
        rPV, rID, rONE = Res(), Res(), Res()
        ch_c = K.chan("ch_const")
        K.dma(SP, ch_c, PV, pvec_d, writes=[rPV])
        ch_c2 = K.chan("ch_const2")
        K.dma(POOL, ch_c2, IDB, ident_d, writes=[rID])
        K.op(DVE, lambda: nc.vector.memset(ONEB, 1.0), writes=[rONE])
        K.op(DVE, lambda: nc.vector.memset(PV[:, 92:93], EPS), reads=[], writes=[rPV])
        K.op(DVE, lambda: nc.vector.memset(PV[:, 93:94], 1.0), reads=[], writes=[rPV])
        K.op(ACT, lambda: nc.scalar.activation(out=PV[:, 80:88], in_=PV[:, 60:68], func=AF.Exp, scale=-1.0),
             reads=[rPV], writes=[rPV])
        K.op(ACT, lambda: nc.scalar.activation(out=PV[:, 80:88], in_=PV[:, 80:88], func=AF.Ln,
                                               bias=PV[:, 93:94], scale=1.0), reads=[rPV], writes=[rPV])
        K.op(DVE, lambda: nc.vector.tensor_scalar(out=PV[:, 80:88], in0=PV[:, 80:88], scalar1=-8.0,
                                                  scalar2=None, op0=ALU.mult), reads=[rPV], writes=[rPV])
        K.op(DVE, lambda: nc.vector.tensor_tensor(out=PV[:, 88:92], in0=PV[:, 68:72], in1=PV[:, 76:80],
                                                  op=ALU.mult), reads=[rPV], writes=[rPV])
        K.barrier()
        EPSC = PV[:, 92:93]
        ONEC = PV[:, 93:94]

        def rows(dram, tok0, n=128):
            return dram[tok0:tok0 + n, :]

        def norm_block(Xb, rXb, Hb, rHb, ST, rst, b):
            K.op(ACT, lambda: nc.scalar.activation(out=Hb, in_=Xb, func=AF.Square, accum_out=ST[:, b:b + 1]),
                 reads=[rXb], writes=[rst[b], rHb], n=1024)
            K.op(ACT, lambda: nc.scalar.activation(out=ST[:, 4 + b:5 + b], in_=ST[:, b:b + 1], func=AF.Sqrt,
                                                   bias=EPSC, scale=1.0 / D),
                 reads=[rst[b]], writes=[rst[4 + b]], tbl="sqrt", dur=0.31)
            K.op(DVE, lambda: nc.vector.reciprocal(out=ST[:, 8 + b:9 + b], in_=ST[:, 4 + b:5 + b]),
                 reads=[rst[4 + b]], writes=[rst[8 + b]], dur=0.2)
            K.op(DVE, lambda: nc.vector.tensor_scalar(out=Hb, in0=Xb, scalar1=ST[:, 8 + b:9 + b],
                                                      scalar2=None, op0=ALU.mult),
                 reads=[rXb, rst[8 + b]], writes=[rHb], n=1024, cls="fast")

        def ffn_phase(src, dst, wa_d, wb_d, pgcol, gbi, tag, emit_ht):
            ar.off = persist_off
            W1 = ar.alloc((NKD, 2 * DFF), BF16)
            W2 = ar.alloc((NF, D), BF16)
            GB = ar.alloc((D,), F32)
            XS = [ar.alloc((D,), F32) for _ in range(2)]
            H = ar.alloc((4, D), BF16)
            HT = ar.alloc((NKD, TT), BF16)
            actb_off = ar.off
            ACTB = ar.alloc((NF, TT), BF16)
            SG = [ar.alloc((TT,), F32) for _ in range(2)]
            yb_off = ar.off
            YB = [ar.alloc((D,), F32) for _ in range(2)]
            xr_off = ar.off
            XR = [ar.alloc((D,), F32) for _ in range(2)]
            ST = ar.alloc((16,), F32)
            ST2 = ar.alloc((16,), F32)
            ST4 = ar.alloc((16,), F32)
            H2 = ar.alloc((D,), BF16)
            HTB = ar.alloc((NKD, 128), BF16)
            STG = [ar.view(actb_off + h * DFF, (DFF,), F32)[0] for h in range(2)]
            rW1 = RL(16)
            rW2, rGB, rH2, rHTB = Res(), Res(), Res(), Res()
            rXS, rH, rHT, rACT = RL(2), RL(4), RL(NKD), RL(NF)
            rSG, rXR, rSTG = RL(2), RL(2), RL(2)
            rYB = [RL(2), RL(2)]
            rst, rst2, rst4 = RL(16), RL(16), RL(16)
            chXS = [K.chan("%s_xs%d" % (tag, i)) for i in range(2)]
            chXR = [K.chan("%s_xr%d" % (tag, i)) for i in range(2)]
            chST = [K.chan("%s_st%d" % (tag, i)) for i in range(2)]
            chW = K.chan("%s_w" % tag)
            chGB = K.chan("%s_gb" % tag)
            chHT = K.chan("%s_ht" % tag)
            chSTG = [K.chan("%s_stg%d" % (tag, i)) for i in range(2)]
            PT = [P[6].bitcast(BF16)[:, 0:512], P[7].bitcast(BF16)[:, 0:512]]
            PTX = [P[6].bitcast(BF16), P[7].bitcast(BF16)]
            rPT = [rP[6], rP[7]]
            G, U, Y = [P[0], P[1]], [P[2], P[3]], [P[4], P[5]]
            rG, rU, rY = [rP[0], rP[1]], [rP[2], rP[3]], [rP[4], rP[5]]

            if SCHED_FFN:
                K.begin_record()
            NJB = NF // 2
            rW1p = [RL(NJB), RL(NJB)]
            STGV = [ar.view(yb_off, (NKD, 256), F32)[0], ar.view(xr_off, (NKD, 256), F32)[0]]
            rSTGV = [rYB[0] + rYB[1], list(rXR)]
            K.dma(SP, chGB, GB, gbc_d[:, gbi * D:(gbi + 1) * D], writes=[rGB], nbytes=524288)
            K.op(DVE, lambda: nc.vector.tensor_scalar(out=GB, in0=GB, scalar1=0.5, scalar2=None, op0=ALU.mult),
                 reads=[rGB], writes=[rGB], n=1024, cls="fast")
            pi = 0
            for jb in range(NJB):
                for h in range(2):
                    sl = pi % 2
                    col = h * DFF + jb * 256
                    K.dma(SP, chSTG[sl], STGV[sl], wa_d[:, col:col + 256].rearrange("(k p) c -> p k c", p=128),
                          writes=list(rSTGV[sl]), nbytes=1048576)
                    for k in range(NKD):
                        if k % 2 == 0:
                            K.op(DVE, lambda k=k, sl=sl, col=col: nc.vector.tensor_scalar(
                                out=W1[:, k, col:col + 256], in0=STGV[sl][:, k, :],
                                scalar1=PV[:, pgcol + k:pgcol + k + 1], scalar2=None, op0=ALU.mult),
                                reads=list(rSTGV[sl]) + [rPV], writes=[rW1p[h][jb]], n=256, cls="fast")
                        else:
                            K.op(ACT, lambda k=k, sl=sl, col=col: nc.scalar.activation(
                                out=W1[:, k, col:col + 256], in_=STGV[sl][:, k, :], func=AF.Copy,
                                scale=PV[:, pgcol + k:pgcol + k + 1]),
                                reads=list(rSTGV[sl]) + [rPV], writes=[rW1p[h][jb]], n=256)
                    pi += 1
            for f0 in range(0, NF, 2):
                K.dma(POOL, chW, W2[:, f0:f0 + 2, :],
                      wb_d[f0 * 128:(f0 + 2) * 128, :].rearrange("(f p) d -> p f d", p=128),
                      reads=[rW1p[1][NJB // 2]], writes=[rW2], nbytes=1048576)

            ntile = ntok // TT

            def do_pre_block(t, b):
                slot = b % 2
                K.dma(SP, chXS[slot], XS[slot], rows(src, t * TT + b * 128), writes=[rXS[slot]])
                norm_block(XS[slot], rXS[slot], H[:, b, :], rH[b], ST, rst, b)

            def do_T(t):
                for k in range(NKD):
                    pt = PT[k % 2]

                    def f(k=k, pt=pt):
                        ins = None
                        for b in range(4):
                            ins = nc.tensor.transpose(out=pt[:, b * 128:(b + 1) * 128],
                                                      in_=H[:, b, k * 128:(k + 1) * 128], identity=IDB)
                        return ins
                    K.op(PE, f, reads=list(rH) + [rID], writes=[rPT[k % 2]], dur=0.5)
                    if k % 2 == 0:
                        K.op(ACT, lambda k=k, pt=pt: nc.scalar.copy(out=HT[:, k, :], in_=pt),
                             reads=[rPT[k % 2]], writes=[rHT[k]])
                    else:
                        K.op(DVE, lambda k=k, pt=pt: nc.vector.tensor_copy(out=HT[:, k, :], in_=pt),
                             reads=[rPT[k % 2]], writes=[rHT[k]])

            pending = []

            def emit_ht_T():
                tok0, par = pending.pop(0)
                ptx = PTX[par]

                def f():
                    ins = None
                    for k in range(NKD):
                        ins = nc.tensor.transpose(out=ptx[:, k * 128:(k + 1) * 128],
                                                  in_=H2[:, k * 128:(k + 1) * 128], identity=IDB)
                    return ins
                K.op(PE, f, reads=[rH2, rID], writes=[rPT[par]], dur=0.9)
                K.op(DVE, lambda: nc.vector.tensor_copy(out=HTB, in_=ptx.rearrange("p (k n) -> p k n", k=NKD)),
                     reads=[rPT[par]], writes=[rHTB], n=1024, cls="fast")
                K.dma(SP, chHT, ht_d[:, :, tok0:tok0 + 128].rearrange("k p n -> p k n"), HTB, reads=[rHTB])

            for b in range(4):
                do_pre_block(0, b)
            do_T(0)
            blkctr = 0

            for t in range(ntile):
                pre_at = {3: 0, 8: 1, 13: 2, 18: 3} if t + 1 < ntile else {}
                for j in range(NF):
                    def fg(j=j, col=j * 128, bank=G[j % 2]):
                        ins = None
                        for k in range(NKD):
                            ins = nc.tensor.matmul(bank, lhsT=W1[:, k, col:col + 128], rhs=HT[:, k, :],
                                                   start=(k == 0), stop=(k == NKD - 1))
                        return ins
                    K.op(PE, fg, reads=rHT + [rW1p[0][j // 2]], writes=[rG[j % 2]], dur=1.75)

                    def fu(j=j, col=DFF + j * 128, bank=U[j % 2]):
                        ins = None
                        for k in range(NKD):
                            ins = nc.tensor.matmul(bank, lhsT=W1[:, k, col:col + 128], rhs=HT[:, k, :],
                                                   start=(k == 0), stop=(k == NKD - 1))
                        return ins
                    K.op(PE, fu, reads=rHT + [rW1p[1][j // 2]], writes=[rU[j % 2]], dur=1.75)
                    K.op(ACT, lambda j=j: nc.scalar.activation(out=SG[j % 2], in_=G[j % 2], func=AF.Silu),
                         reads=[rG[j % 2]], writes=[rSG[j % 2]], tbl="silu")
                    K.op(DVE, lambda j=j: nc.vector.tensor_tensor(out=ACTB[:, j, :], in0=U[j % 2], in1=SG[j % 2],
                                                                  op=ALU.mult),
                         reads=[rU[j % 2], rSG[j % 2]], writes=[rACT[j]])
                    if j == 1 and pending:
                        emit_ht_T()
                    if j in pre_at:
                        do_pre_block(t + 1, pre_at[j])
                if t + 1 < ntile:
                    do_T(t + 1)
                for b in range(2):
                    K.dma(SP, chXR[b], XR[b], rows(src, t * TT + b * 128), writes=[rXR[b]])
                for b in range(4):
                    slot = b % 2
                    for hf in range(2):
                        def fy(b=b, hf=hf):
                            ins = None
                            for f in range(NF):
                                ins = nc.tensor.matmul(Y[hf], lhsT=ACTB[:, f, b * 128:(b + 1) * 128],
                                                       rhs=W2[:, f, hf * 512:(hf + 1) * 512],
                                                       start=(f == 0), stop=(f == NF - 1))
                            return ins
                        K.op(PE, fy, reads=rACT + [rW2], writes=[rY[hf]], dur=4.8)
                        if hf == 0:
                            K.op(ACT, lambda slot=slot, hf=hf: nc.scalar.copy(
                                out=YB[slot][:, hf * 512:(hf + 1) * 512], in_=Y[hf]),
                                reads=[rY[hf]], writes=[rYB[slot][hf]])
                        else:
                            K.op(DVE, lambda slot=slot, hf=hf: nc.vector.tensor_copy(
                                out=YB[slot][:, hf * 512:(hf + 1) * 512], in_=Y[hf]),
                                reads=[rY[hf]], writes=[rYB[slot][hf]])
                    if pending:
                        emit_ht_T()
                    K.op(ACT, lambda slot=slot, b=b: nc.scalar.activation(
                        out=SG[0].bitcast(BF16), in_=YB[slot], func=AF.Square, accum_out=ST2[:, b:b + 1]),
                        reads=rYB[slot], writes=[rst2[b], rSG[0]], n=1024)
                    K.op(ACT, lambda b=b: nc.scalar.activation(out=ST2[:, 4 + b:5 + b], in_=ST2[:, b:b + 1],
                                                               func=AF.Sqrt, bias=EPSC, scale=1.0 / D),
                         reads=[rst2[b]], writes=[rst2[4 + b]], tbl="sqrt", dur=0.31)
                    K.op(DVE, lambda b=b: nc.vector.reciprocal(out=ST2[:, 8 + b:9 + b], in_=ST2[:, 4 + b:5 + b]),
                         reads=[rst2[4 + b]], writes=[rst2[8 + b]], dur=0.2)
                    K.op(DVE, lambda slot=slot, b=b: nc.vector.scalar_tensor_tensor(
                        out=YB[slot], in0=YB[slot], scalar=ST2[:, 8 + b:9 + b], in1=GB,
                        op0=ALU.mult, op1=ALU.mult),
                        reads=rYB[slot] + [rst2[8 + b], rGB], writes=rYB[slot], n=1024)
                    K.op(POOL, lambda slot=slot: nc.gpsimd.tensor_tensor(out=XR[slot], in0=YB[slot], in1=XR[slot],
                                                                         op=ALU.add),
                         reads=rYB[slot] + [rXR[slot]], writes=[rXR[slot]], n=1024)
                    tok0 = t * TT + b * 128
                    K.dma(SP, chST[slot], rows(dst, tok0), XR[slot], reads=[rXR[slot]])
                    if emit_ht:
                        norm_block(XR[slot], rXR[slot], H2, rH2, ST4, rst4, b)
                        pending.append((tok0, blkctr % 2))
                        blkctr += 1
                    if b + 2 < 4:
                        K.dma(SP, chXR[slot], XR[slot], rows(src, t * TT + (b + 2) * 128), writes=[rXR[slot]])
            while pending:
                emit_ht_T()
            if SCHED_FFN:
                K.flush(window=SCHED_WINDOW, verbose=True)
            K.barrier()

        def mixer_phase(src, dst):
            ar.off = persist_off
            WIN = ar.alloc((NKD, 1536), BF16)
            WOUT = ar.alloc((NKD, D), BF16)
            GW = ar.alloc((16, 128), BF16)
            PWT = ar.alloc((4, 128), BF16)
            GBm = ar.alloc((D,), F32)
            CB = ar.alloc((NT, 4), F32)
            CF = ar.alloc((4,), F32)
            CBR = ar.alloc((4,), F32)
            sub_off = ar.off
            HW = [ar.alloc((NKD, 528), BF16) for _ in range(2)]
            ZX = ar.alloc((4, 528), F32)
            ZP = ar.alloc((4, 528), F32)
            GEL = ar.alloc((4, TT), F32)
            XC = ar.alloc((4, TT), F32)
            XCB = ar.alloc((4, TT), BF16)
            ta_off = ar.off
            TA = [[ar.alloc((TT,), F32) for _ in range(3)] for _ in range(8)]
            HF = ar.alloc((4, TT), F32)
            VB = ar.alloc((4, TT), BF16)
            PB = ar.alloc((4, TT), BF16)
            SQ = [ar.alloc((TT,), BF16) for _ in range(4)]
            PA = [ar.alloc((528,), F32) for _ in range(2)]
            DD = ar.alloc((4, TT), BF16)
            O = [ar.alloc((D,), F32) for _ in range(2)]
            XR = [ar.alloc((D,), F32) for _ in range(2)]
            ST3 = ar.alloc((40,), F32)
            rWIN, rWOUT, rGW, rPWT, rGBm = Res(), Res(), Res(), Res(), Res()
            rCB, rCF, rCBR = RL(NT), RL(4), RL(4)
            rHW = RL(2)
            rZX, rZP, rGEL, rXC, rXCB = RL(4), RL(4), RL(4), RL(4), RL(4)
            rTA = [RL(3) for _ in range(8)]
            rHF, rVB, rPB, rDD = RL(4), RL(4), RL(4), RL(4)
            rSQ, rPA, rXR = RL(4), RL(2), RL(2)
            rO = [RL(2), RL(2)]
            rst3 = RL(40)

            def ta_view(j0):
                v, _ = ar.view(ta_off + j0 * TT, (4, TT), F32)
                return v, [rTA[j // 3][j % 3] for j in range(j0, j0 + 4)]
            HBS0, rHBS0 = ta_view(12)
            HBS1, rHBS1 = ta_view(16)
            XC2, rXC2 = ta_view(20)
            HBS, rHBS = [HBS0, HBS1], [rHBS0, rHBS1]
            XCS, rXCS = [XC, XC2], [rXC, rXC2]
            chXCS = [K.chan("mx_xcs%d" % i) for i in range(2)]
            chHBS = [K.chan("mx_hbs%d" % i) for i in range(8)]
            chXCL = [K.chan("mx_xcl%d" % i) for i in range(2)]
            chHBL = [K.chan("mx_hbl%d" % i) for i in range(2)]
            chW = K.chan("mx_w")
            chSTG = K.chan("mx_stg")
            chGBm = K.chan("mx_gb")
            chHW = [K.chan("mx_hw%d" % i) for i in range(2)]
            chXR = [K.chan("mx_xr%d" % i) for i in range(2)]
            chST = [K.chan("mx_st%d" % i) for i in range(2)]
            STG, _ = ar.view(sub_off, (1536,), F32)
            rSTG = Res()
            for k in range(NKD):
                K.dma(SP, chSTG, STG, win_d[k * 128:(k + 1) * 128, :], writes=[rSTG])
                K.op(DVE, lambda k=k: nc.vector.tensor_scalar(out=WIN[:, k, :], in0=STG,
                                                              scalar1=PV[:, 8 + k:9 + k], scalar2=None,
                                                              op0=ALU.mult),
                     reads=[rSTG, rPV], writes=[rWIN])
            for k in range(NKD):
                K.dma(SP, chSTG, STG[:, 0:D], wout_d[k * 128:(k + 1) * 128, :], writes=[rSTG])
                K.op(DVE, lambda k=k: nc.vector.tensor_scalar(out=WOUT[:, k, :], in0=STG[:, 0:D],
                                                              scalar1=PV[:, 72 + k:73 + k], scalar2=None,
                                                              op0=ALU.mult),
                     reads=[rSTG, rPV], writes=[rWOUT])
            K.dma(POOL, chW, GW, gw_d.rearrange("p (a b) -> p a b", a=16), writes=[rGW])
            K.dma(POOL, chW, PWT, pw_d.rearrange("p (a b) -> p a b", a=4), writes=[rPWT])
            K.dma(SP, chGBm, GBm, gbc_d[:, D:2 * D], writes=[rGBm])
            K.barrier()

            ZM, rZM = [P[0], P[1]], [rP[0], rP[1]]
            ZH, rZH = P[2], rP[2]
            GPS = [([P[3], P[4]], [rP[3], rP[4]]), ([P[5], P[6]], [rP[5], rP[6]])]
            OL, rOL = P[5], rP[5]
            OP, rOP = P[6], rP[6]
            SSB, rSSB = P[7], rP[7]
            state = {"hw": 0, "zi": 0, "sq": 0, "b2": 0}

            def load_window(tok_s, t):
                slot = state["hw"] % 2
                state["hw"] += 1
                hw = HW[slot]
                c0 = tok_s + t * TT
                lo = 0 if t == 0 else 8
                hi = 0 if t == NT - 1 else 8
                K.dma(SP, chHW[slot], hw[:, :, 8 - lo:520 + hi],
                      ht_d[:, :, c0 - lo:c0 + TT + hi].rearrange("k p n -> p k n"), writes=[rHW[slot]], nbytes=1081344)
                if lo == 0:
                    K.op(POOL, lambda: nc.gpsimd.memset(hw[:, :, 0:8], 0.0), writes=[rHW[slot]], dur=0.3)
                if hi == 0:
                    K.op(POOL, lambda: nc.gpsimd.memset(hw[:, :, 520:528], 0.0), writes=[rHW[slot]], dur=0.3)
                return slot

            def inproj_chunk(slot, oc, halo):
                hw = HW[slot]
                zi = state["zi"]
                state["zi"] += 1
                zm, rzm = ZM[zi % 2], rZM[zi % 2]

                def fm():
                    ins = None
                    for k in range(NKD):
                        ins = nc.tensor.matmul(zm, lhsT=WIN[:, k, oc * 128:(oc + 1) * 128],
                                               rhs=hw[:, k, 8:520], start=(k == 0), stop=(k == NKD - 1))
                    return ins
                K.op(PE, fm, reads=[rHW[slot], rWIN], writes=[rzm], dur=1.76)
                if halo:
                    for side in range(2):
                        hc = 0 if side == 0 else 520

                        def fh(side=side, hc=hc):
                            ins = None
                            for k in range(NKD):
                                ins = nc.tensor.matmul(
                                    ZH[:, oc * 16 + side * 8:oc * 16 + side * 8 + 8],
                                    lhsT=WIN[:, k, oc * 128:(oc + 1) * 128], rhs=hw[:, k, hc:hc + 8],
                                    start=(k == 0), stop=(k == NKD - 1))
                            return ins
                        K.op(PE, fh, reads=[rHW[slot], rWIN], writes=[rZH], dur=0.56)
                return zm, rzm

            def evac_window(Z, rZc, c, oc, zm, rzm, main_eng):
                if main_eng is ACT:
                    K.op(ACT, lambda: nc.scalar.copy(out=Z[:, c, 8:520], in_=zm), reads=[rzm], writes=[rZc[c]])
                else:
                    K.op(DVE, lambda: nc.vector.tensor_copy(out=Z[:, c, 8:520], in_=zm), reads=[rzm],
                         writes=[rZc[c]])
                K.op(DVE, lambda: nc.vector.tensor_copy(out=Z[:, c, 0:8], in_=ZH[:, oc * 16:oc * 16 + 8]),
                     reads=[rZH], writes=[rZc[c]])
                K.op(DVE, lambda: nc.vector.tensor_copy(out=Z[:, c, 520:528], in_=ZH[:, oc * 16 + 8:oc * 16 + 16]),
                     reads=[rZH], writes=[rZc[c]], dur=0.17)

            FS0 = (ZX, rZX, XC, rXC, XCB, rXCB)
            FS1 = (ZP, rZP, GEL, rGEL, VB, rVB)

            def conv_chunk(c, fs):
                Zf, rZf, XCf, rXCf, XCBf, rXCBf = fs
                K.op(ACT, lambda: nc.scalar.activation(
                    out=XCf[:, c, :], in_=Zf[:, c, 6:6 + TT], func=AF.Identity, scale=PV[:, 24 + c:25 + c],
                    bias=PV[:, 40 + c:41 + c]),
                    reads=[rZf[c], rPV], writes=[rXCf[c]])
                for kk in range(1, 4):
                    K.op(DVE, lambda kk=kk: nc.vector.scalar_tensor_tensor(
                        out=XCf[:, c, :], in0=Zf[:, c, 6 + kk:6 + kk + TT],
                        scalar=PV[:, 24 + kk * 4 + c:25 + kk * 4 + c], in1=XCf[:, c, :],
                        op0=ALU.mult, op1=ALU.add),
                        reads=[rZf[c], rPV, rXCf[c]], writes=[rXCf[c]])
                K.op(ACT, lambda: nc.scalar.copy(out=XCBf[:, c, :], in_=XCf[:, c, :]),
                     reads=[rXCf[c]], writes=[rXCBf[c]],
                     alts=[(DVE, lambda: nc.vector.tensor_copy(out=XCBf[:, c, :], in_=XCf[:, c, :])),
                           (POOL, lambda: nc.gpsimd.tensor_copy(out=XCBf[:, c, :], in_=XCf[:, c, :]))])

            def lru_stage(k, job, i):
                dr, c = job["dr"], job["c"]
                Zf, rZf, XCf, rXCf, XCBf, rXCBf = job.get("fs", FS0)
                T1, T2, T3 = TA[i]
                r1, r2, r3 = rTA[i]
                gp, rgp = GPS[i % 2]
                if k == 0:
                    ga = GW[:, dr * 8 + c, :]
                    gx = GW[:, dr * 8 + 4 + c, :]
                    K.op(PE, lambda: nc.tensor.matmul(gp[0], lhsT=ga, rhs=XCBf[:, c, :], start=True, stop=True),
                         reads=[rGW, rXCBf[c]], writes=[rgp[0]])
                    K.op(PE, lambda: nc.tensor.matmul(gp[1], lhsT=gx, rhs=XCBf[:, c, :], start=True, stop=True),
                         reads=[rGW, rXCBf[c]], writes=[rgp[1]])
                elif k == 1:
                    ba = PV[:, 44 + dr * 4 + c:45 + dr * 4 + c]
                    bx = PV[:, 52 + dr * 4 + c:53 + dr * 4 + c]
                    K.op(ACT, lambda: nc.scalar.activation(out=T1, in_=gp[0], func=AF.Sigmoid, bias=ba, scale=1.0),
                         reads=[rgp[0], rPV], writes=[r1], tbl="sig")
                    K.op(ACT, lambda: nc.scalar.activation(out=T2, in_=gp[1], func=AF.Sigmoid, bias=bx, scale=1.0),
                         reads=[rgp[1], rPV], writes=[r2], tbl="sig")
                elif k == 2:
                    nsp = PV[:, 80 + dr * 4 + c:81 + dr * 4 + c]
                    K.op(ACT, lambda: nc.scalar.activation(out=T3, in_=T1, func=AF.Exp, scale=nsp),
                         reads=[r1, rPV], writes=[r3], tbl="exp")
                    K.op(POOL, lambda: nc.gpsimd.tensor_tensor(out=T2, in0=T2, in1=XCf[:, c, :], op=ALU.mult),
                         reads=[r2, rXCf[c]], writes=[r2],
                         alts=[(DVE, lambda: nc.vector.tensor_tensor(out=T2, in0=T2, in1=XCf[:, c, :], op=ALU.mult))])
                elif k == 3:
                    K.op(ACT, lambda: nc.scalar.activation(out=T1, in_=T3, func=AF.Square),
                         reads=[r3, r1], writes=[r1],
                         alts=[(POOL, lambda: nc.gpsimd.tensor_tensor(out=T1, in0=T3, in1=T3, op=ALU.mult)),
                               (DVE, lambda: nc.vector.tensor_tensor(out=T1, in0=T3, in1=T3, op=ALU.mult))])
                elif k == 4:
                    K.op(ACT, lambda: nc.scalar.activation(out=T1, in_=T1, func=AF.Sqrt, bias=ONEC, scale=-1.0),
                         reads=[r1, rPV], writes=[r1], tbl="sqrt")
                elif k == 5:
                    K.op(DVE, lambda: nc.vector.tensor_tensor(out=T2, in0=T2, in1=T1, op=ALU.mult),
                         reads=[r1, r2], writes=[r2],
                         alts=[(POOL, lambda: nc.gpsimd.tensor_tensor(out=T2, in0=T2, in1=T1, op=ALU.mult))])
                elif k == 6:
                    out_ap, r_out = job["out"]
                    init_ap, r_init = job["init"]
                    if dr == 0:
                        K.op(DVE, lambda: nc.vector.tensor_tensor_scan(out=out_ap, data0=T3, data1=T2,
                                                                       initial=init_ap, op0=ALU.mult, op1=ALU.add),
                             reads=[r2, r3, r_init], writes=[r_out], cls="scan")
                    else:
                        K.op(DVE, lambda: nc.vector.tensor_tensor_scan(out=out_ap[:, ::-1], data0=T3[:, ::-1],
                                                                       data1=T2[:, ::-1], initial=init_ap,
                                                                       op0=ALU.mult, op1=ALU.add),
                             reads=[r2, r3, r_init], writes=[r_out], cls="scan")
                elif k == 7:
                    job["post"](job, i)

            def lru_A(jobs):
                for i, job in enumerate(jobs):
                    lru_stage(0, job, i)
                    lru_stage(1, job, i)
                for i, job in enumerate(jobs):
                    lru_stage(2, job, i)

            def lru_st(jobs, k0, k1):
                for k in range(k0, k1):
                    for i, job in enumerate(jobs):
                        lru_stage(k, job, i)

            def front(slot, fs=None):
                fs = fs or FS0
                for c in range(4):
                    zm, rzm = inproj_chunk(slot, c, True)
                    evac_window(fs[0], fs[1], c, c, zm, rzm, ACT)
                    conv_chunk(c, fs)

            K.begin_record()
            for s in range(nseq):
                tok_s = s * S
                rXCD = RL(NT)
                rHBD = [RL(4) for _ in range(NT)]
                for c in range(4):
                    K.op(POOL, lambda c=c: nc.gpsimd.memset(CBR[:, c:c + 1], 0.0), writes=[rCBR[c]], dur=0.3)
                b1_tiles = list(range(NT - 1, -1, -1))
                groups = [b1_tiles[i:i + 2] for i in range(0, len(b1_tiles), 2)]
                FSS = [FS0, FS1]
                for grp in groups:
                    slots = [load_window(tok_s, t) for t in grp]
                    for gi, t in enumerate(grp):
                        front(slots[gi], FSS[gi])
                        tok0 = tok_s + t * TT
                        K.dma(SP, chXCS[gi], xc_d[:, :, tok0:tok0 + TT].rearrange("c p n -> p c n"), FSS[gi][2],
                              reads=list(FSS[gi][3]), writes=[rXCD[t]], nbytes=1048576)
                    jobs = []
                    for gi, t in enumerate(grp):
                        for c in range(4):
                            i = gi * 4 + c
                            if gi == 0:
                                init = (CBR[:, c:c + 1], rCBR[c])
                            else:
                                init = (TA[c][0][:, 0:1], rTA[c][0])
                            last = (gi == len(grp) - 1)

                            def post_b1(job, i, t=t, last=last):
                                c = job["c"]
                                T1 = TA[i][0]
                                tok0 = tok_s + t * TT
                                if last:
                                    K.op(ACT, lambda: nc.scalar.copy(out=CBR[:, c:c + 1], in_=T1[:, 0:1]),
                                         reads=[rTA[i][0]], writes=[rCBR[c]], dur=0.31)
                                K.dma(SP, chHBS[i], hb_d[c, :, tok0:tok0 + TT], T1,
                                      reads=[rTA[i][0]], writes=[rHBD[t][c]], nbytes=262144)
                            jobs.append(dict(dr=1, c=c, init=init, out=(TA[i][0], rTA[i][0]), post=post_b1,
                                             fs=FSS[gi]))
                    lru_A(jobs)
                    lru_st(jobs, 3, 8)

                for c in range(4):
                    K.op(POOL, lambda c=c: nc.gpsimd.memset(CF[:, c:c + 1], 0.0), writes=[rCF[c]], dur=0.3)

                def load_b2(t):
                    sl = state["b2"] % 2
                    state["b2"] += 1
                    tok0 = tok_s + t * TT
                    K.dma(SP, chXCL[sl], XCS[sl], xc_d[:, :, tok0:tok0 + TT].rearrange("c p n -> p c n"),
                          reads=[rXCD[t]], writes=list(rXCS[sl]), nbytes=1048576)
                    K.dma(SP, chHBL[sl], HBS[sl], hb_d[:, :, tok0:tok0 + TT].rearrange("c p n -> p c n"),
                          reads=list(rHBD[t]), writes=list(rHBS[sl]), nbytes=1048576)
                    return sl
                slot = load_window(tok_s, 0)
                bsl = load_b2(0)
                pend = []
                for t in range(NT):
                    if t + 1 < NT:
                        nslot = load_window(tok_s, t + 1)
                        nbsl = load_b2(t + 1)
                    XCt, rXCt = XCS[bsl], rXCS[bsl]
                    HBt, rHBt = HBS[bsl], rHBS[bsl]
                    for c in range(4):
                        K.op(ACT, lambda c=c, XCt=XCt: nc.scalar.copy(out=XCB[:, c, :], in_=XCt[:, c, :]),
                             reads=[rXCt[c]], writes=[rXCB[c]],
                             alts=[(DVE, lambda c=c, XCt=XCt: nc.vector.tensor_copy(out=XCB[:, c, :], in_=XCt[:, c, :])),
                                   (POOL, lambda c=c, XCt=XCt: nc.gpsimd.tensor_copy(out=XCB[:, c, :],
                                                                                      in_=XCt[:, c, :]))])
                    for c in range(4):
                        zm, rzm = inproj_chunk(slot, 4 + c, False)
                        K.op(ACT, lambda c=c, zm=zm: nc.scalar.activation(out=GEL[:, c, :], in_=zm,
                                                                          func=AF.Gelu_apprx_tanh),
                             reads=[rzm], writes=[rGEL[c]], tbl="gelu")

                    def post_f(job, i, HBt=HBt, rHBt=rHBt):
                        c = job["c"]
                        K.op(ACT, lambda: nc.scalar.copy(out=CF[:, c:c + 1], in_=HF[:, c, TT - 1:TT]),
                             reads=[rHF[c]], writes=[rCF[c]], dur=0.31)
                        K.op(DVE, lambda: nc.vector.tensor_tensor(out=HF[:, c, :], in0=HF[:, c, :], in1=HBt[:, c, :],
                                                                  op=ALU.add),
                             reads=[rHF[c], rHBt[c]], writes=[rHF[c]],
                             alts=[(POOL, lambda: nc.gpsimd.tensor_tensor(out=HF[:, c, :], in0=HF[:, c, :],
                                                                          in1=HBt[:, c, :], op=ALU.add))])
                        K.op(POOL, lambda: nc.gpsimd.tensor_tensor(out=VB[:, c, :], in0=HF[:, c, :], in1=GEL[:, c, :],
                                                                   op=ALU.mult),
                             reads=[rHF[c], rGEL[c]], writes=[rVB[c]],
                             alts=[(DVE, lambda: nc.vector.tensor_tensor(out=VB[:, c, :], in0=HF[:, c, :],
                                                                         in1=GEL[:, c, :], op=ALU.mult))])
                        sqi = state["sq"] % 4
                        state["sq"] += 1
                        sq, rsq = SQ[sqi], rSQ[sqi]
                        K.op(ACT, lambda: nc.scalar.activation(out=sq, in_=VB[:, c, :], func=AF.Square),
                             reads=[rVB[c]], writes=[rsq],
                             alts=[(POOL, lambda: nc.gpsimd.tensor_tensor(out=sq, in0=VB[:, c, :], in1=VB[:, c, :],
                                                                          op=ALU.mult)),
                                   (DVE, lambda: nc.vector.tensor_tensor(out=sq, in0=VB[:, c, :], in1=VB[:, c, :],
                                                                         op=ALU.mult))])

                        def fss():
                            ins = None
                            for blk in range(4):
                                col = blk * 4 + c
                                ins = nc.tensor.matmul(SSB[:, col:col + 1], lhsT=sq[:, blk * 128:(blk + 1) * 128],
                                                       rhs=ONEB[:, 0:1], start=True, stop=True)
                            return ins
                        K.op(PE, fss, reads=[rsq, rONE], writes=[rSSB], dur=0.3)

                    fsb = (None, None, XCt, rXCt, XCB, rXCB)
                    jobs = []
                    for c in range(4):
                        jobs.append(dict(dr=0, c=c, init=(CF[:, c:c + 1], rCF[c]), out=(HF[:, c, :], rHF[c]),
                                         post=post_f, fs=fsb))
                    lru_A(jobs)
                    if pend:
                        pend.pop(0)()
                    for c in range(4):
                        zm, rzm = inproj_chunk(slot, 8 + c, True)
                        evac_window(ZP, rZP, c, 8 + c, zm, rzm, ACT)
                    lru_st(jobs, 3, 4)

                    for g in range(4):
                        w = WINS[g]
                        A, B = PA
                        rA, rB = rPA
                        K.op(POOL, lambda g=g: nc.gpsimd.tensor_tensor(out=A[:, 0:527], in0=ZP[:, g, 0:527],
                                                                       in1=ZP[:, g, 1:528], op=ALU.add),
                             reads=[rZP[g]], writes=[rA],
                             alts=[(DVE, lambda g=g: nc.vector.tensor_tensor(out=A[:, 0:527], in0=ZP[:, g, 0:527],
                                                                             in1=ZP[:, g, 1:528], op=ALU.add))])
                        cur, rcur, oth, roth = A, rA, B, rB
                        n = 527
                        step = 2
                        for lvl in range(g):
                            n2 = n - step
                            K.op(POOL, lambda cur=cur, oth=oth, n2=n2, step=step: nc.gpsimd.tensor_tensor(
                                out=oth[:, 0:n2], in0=cur[:, 0:n2], in1=cur[:, step:step + n2], op=ALU.add),
                                reads=[rcur], writes=[roth],
                                alts=[(DVE, lambda cur=cur, oth=oth, n2=n2, step=step: nc.vector.tensor_tensor(
                                    out=oth[:, 0:n2], in0=cur[:, 0:n2], in1=cur[:, step:step + n2], op=ALU.add))])
                            cur, rcur, oth, roth = oth, roth, cur, rcur
                            n = n2
                            step *= 2
                        off = 8 - w // 2
                        K.op(DVE, lambda g=g, cur=cur, off=off, w=w: nc.vector.scalar_tensor_tensor(
                            out=DD[:, g, :], in0=cur[:, off:off + TT], scalar=1.0 / w, in1=ZP[:, g, 8:8 + TT],
                            op0=ALU.mult, op1=ALU.subtract),
                            reads=[rcur, rZP[g]], writes=[rDD[g]])
                        fixes = []
                        if t == 0:
                            for tk in range(w // 2):
                                fixes.append((tk, tk + w // 2))
                        if t == NT - 1:
                            for m in range(1, w // 2):
                                fixes.append((TT - m, w // 2 + m))
                        for (col, cnt) in fixes:
                            K.op(DVE, lambda g=g, cur=cur, off=off, col=col, cnt=cnt: nc.vector.scalar_tensor_tensor(
                                out=DD[:, g, col:col + 1], in0=cur[:, off + col:off + col + 1], scalar=1.0 / cnt,
                                in1=ZP[:, g, 8 + col:9 + col], op0=ALU.mult, op1=ALU.subtract),
                                reads=[rcur, rZP[g], rDD[g]], writes=[rDD[g]], dur=0.17)
                        gp = ZM[g % 2]
                        rgp = rZM[g % 2]
                        K.op(PE, lambda g=g, gp=gp: nc.tensor.matmul(gp, lhsT=PWT[:, g, :], rhs=DD[:, g, :],
                                                                     start=True, stop=True),
                             reads=[rPWT, rDD[g]], writes=[rgp])
                        K.op(ACT, lambda g=g, gp=gp: nc.scalar.activation(out=PB[:, g, :], in_=gp, func=AF.Copy,
                                                                          scale=PV[:, 68 + g:69 + g]),
                             reads=[rgp, rPV], writes=[rPB[g]])
                        sqi = state["sq"] % 4
                        state["sq"] += 1
                        sq, rsq = SQ[sqi], rSQ[sqi]
                        K.op(ACT, lambda g=g, sq=sq: nc.scalar.activation(out=sq, in_=PB[:, g, :], func=AF.Square),
                             reads=[rPB[g]], writes=[rsq],
                             alts=[(POOL, lambda g=g, sq=sq: nc.gpsimd.tensor_tensor(out=sq, in0=PB[:, g, :],
                                                                                    in1=PB[:, g, :], op=ALU.mult)),
                                   (DVE, lambda g=g, sq=sq: nc.vector.tensor_tensor(out=sq, in0=PB[:, g, :],
                                                                                   in1=PB[:, g, :], op=ALU.mult))])

                        def fsp(g=g, sq=sq):
                            ins = None
                            for blk in range(4):
                                col = 16 + blk * 4 + g
                                ins = nc.tensor.matmul(SSB[:, col:col + 1], lhsT=sq[:, blk * 128:(blk + 1) * 128],
                                                       rhs=ONEB[:, 0:1], start=True, stop=True)
                            return ins
                        K.op(PE, fsp, reads=[rsq, rONE], writes=[rSSB], dur=0.3)
                    lru_st(jobs, 4, 8)
                    K.op(DVE, lambda: nc.vector.tensor_reduce(
                        out=ST3[:, 0:8], in_=SSB[:, 0:32].rearrange("p (a b) -> p a b", b=4), axis=AX.X, op=ALU.add),
                        reads=[rSSB], writes=[rst3[0]], dur=0.2)
                    K.op(ACT, lambda: nc.scalar.activation(out=ST3[:, 8:16], in_=ST3[:, 0:8], func=AF.Sqrt,
                                                           bias=EPSC, scale=1.0 / 512),
                         reads=[rst3[0]], writes=[rst3[1]], tbl="sqrt", dur=0.31)
                    K.op(DVE, lambda: nc.vector.reciprocal(out=ST3[:, 16:24], in_=ST3[:, 8:16]),
                         reads=[rst3[1]], writes=[rst3[2]], dur=0.2)
                    def s6(t=t):
                        for b in range(2):
                            K.dma(SP, chXR[b], XR[b], rows(src, tok_s + t * TT + b * 128), writes=[rXR[b]])
                        for b in range(4):
                            slot2 = b % 2
                            for hf in range(2):
                                def fol(b=b, hf=hf):
                                    ins = None
                                    for c in range(4):
                                        ins = nc.tensor.matmul(OL, lhsT=VB[:, c, b * 128:(b + 1) * 128],
                                                               rhs=WOUT[:, c, hf * 512:(hf + 1) * 512],
                                                               start=(c == 0), stop=(c == 3))
                                    return ins
                                K.op(PE, fol, reads=rVB + [rWOUT], writes=[rOL], dur=0.88)

                                def fop(b=b, hf=hf):
                                    ins = None
                                    for g in range(4):
                                        ins = nc.tensor.matmul(OP, lhsT=PB[:, g, b * 128:(b + 1) * 128],
                                                               rhs=WOUT[:, 4 + g, hf * 512:(hf + 1) * 512],
                                                               start=(g == 0), stop=(g == 3))
                                    return ins
                                K.op(PE, fop, reads=rPB + [rWOUT], writes=[rOP], dur=0.88)
                                oh = O[slot2][:, hf * 512:(hf + 1) * 512]
                                K.op(ACT, lambda oh=oh, b=b: nc.scalar.activation(out=oh, in_=OL, func=AF.Copy,
                                                                                  scale=ST3[:, 16 + b:17 + b]),
                                     reads=[rOL, rst3[2]], writes=[rO[slot2][hf]])
                                K.op(DVE, lambda oh=oh, b=b: nc.vector.scalar_tensor_tensor(
                                    out=oh, in0=OP, scalar=ST3[:, 20 + b:21 + b], in1=oh, op0=ALU.mult, op1=ALU.add),
                                    reads=[rOP, rst3[2], rO[slot2][hf]], writes=[rO[slot2][hf]])
                            K.op(ACT, lambda slot2=slot2, b=b: nc.scalar.activation(
                                out=ZX[:, 0, 0:512].bitcast(BF16), in_=O[slot2], func=AF.Square,
                                accum_out=ST3[:, 24 + b:25 + b]),
                                reads=rO[slot2], writes=[rst3[3 + b], rZX[0]], n=1024)
                            K.op(ACT, lambda b=b: nc.scalar.activation(out=ST3[:, 28 + b:29 + b], in_=ST3[:, 24 + b:25 + b],
                                                                       func=AF.Sqrt, bias=EPSC, scale=1.0 / D),
                                 reads=[rst3[3 + b]], writes=[rst3[7 + b]], tbl="sqrt", dur=0.31)
                            K.op(DVE, lambda b=b: nc.vector.reciprocal(out=ST3[:, 32 + b:33 + b], in_=ST3[:, 28 + b:29 + b]),
                                 reads=[rst3[7 + b]], writes=[rst3[11 + b]], dur=0.2)
                            K.op(DVE, lambda slot2=slot2, b=b: nc.vector.scalar_tensor_tensor(
                                out=O[slot2], in0=O[slot2], scalar=ST3[:, 32 + b:33 + b], in1=GBm,
                                op0=ALU.mult, op1=ALU.mult),
                                reads=rO[slot2] + [rst3[11 + b], rGBm], writes=rO[slot2], n=1024)
                            K.op(POOL, lambda slot2=slot2: nc.gpsimd.tensor_tensor(out=XR[slot2], in0=O[slot2],
                                                                                   in1=XR[slot2], op=ALU.add),
                                 reads=rO[slot2] + [rXR[slot2]], writes=[rXR[slot2]], n=1024,
                                 alts=[(DVE, lambda slot2=slot2: nc.vector.tensor_tensor(
                                     out=XR[slot2], in0=O[slot2], in1=XR[slot2], op=ALU.add))])
                            K.dma(SP, chST[slot2], rows(dst, tok_s + t * TT + b * 128), XR[slot2], reads=[rXR[slot2]])
                            if b + 2 < 4:
                                K.dma(SP, chXR[slot2], XR[slot2], rows(src, tok_s + t * TT + (b + 2) * 128),
                                      writes=[rXR[slot2]])
                    pend.append(s6)
                    if t + 1 < NT:
                        slot = nslot
                        bsl = nbsl
                while pend:
                    pend.pop(0)()
            K.flush(window=SCHED_WINDOW, verbose=True)
            K.barrier()

        ffn_phase(x_d, s1_d, w1a_d, w1b_d, 0, 0, "fa", True)
        mixer_phase(s1_d, s2_d)
        ffn_phase(s2_d, y_d, w2a_d, w2b_d, 16, 2, "fc", False)
        K.barrier()
    return nc


def host_prep(inputs, nseq_total):
    f = lambda a: np.ascontiguousarray(np.asarray(a, dtype=np.float32))
    pv = np.zeros((128, NPV), np.float32)

    def put(col0, vec, n):
        pv[:, col0:col0 + n] = f(vec).reshape(n, 128).T

    put(0, inputs["ffn1_pre_g"][0], 8)
    put(8, inputs["mix_pre_g"][0], 8)
    put(16, inputs["ffn2_pre_g"][0], 8)
    cw = f(inputs["conv_w"][0])
    for k in range(4):
        put(24 + 4 * k, cw[k], 4)
    put(40, inputs["conv_b"][0], 4)
    for dr in range(2):
        put(44 + 4 * dr, inputs["lru_b_a"][0][dr], 4)
        put(52 + 4 * dr, inputs["lru_b_x"][0][dr], 4)
        put(60 + 4 * dr, inputs["lru_lam"][0][dr], 4)
    put(68, inputs["pool_scale"][0], 4)
    put(72, inputs["lru_out_g"][0], 4)
    put(76, inputs["pool_out_g"][0], 4)
    gw = np.zeros((128, 16, 128), np.float32)
    wa = f(inputs["lru_w_a"][0])
    wx = f(inputs["lru_w_x"][0])
    for dr in range(2):
        for gi, wsrc in enumerate((wa, wx)):
            for c in range(4):
                for hh in range(2):
                    gw[hh * 64:(hh + 1) * 64, dr * 8 + gi * 4 + c, hh * 64:(hh + 1) * 64] = wsrc[dr, 2 * c + hh]
    pw = np.ascontiguousarray(f(inputs["pool_w"][0]).transpose(1, 0, 2))
    gbc = np.zeros((128, 3, D), np.float32)
    gbc[:, 0, :] = f(inputs["ffn1_post_g"][0])[None, :]
    gbc[:, 1, :] = f(inputs["mix_post_g"][0])[None, :]
    gbc[:, 2, :] = f(inputs["ffn2_post_g"][0])[None, :]
    shared = {
        "w1a": f(inputs["ffn1_w_in"][0]), "w1b": f(inputs["ffn1_w_out"][0]),
        "w2a": f(inputs["ffn2_w_in"][0]), "w2b": f(inputs["ffn2_w_out"][0]),
        "win": f(inputs["w_in"][0]), "wout": f(inputs["w_out"][0]),
        "gw": gw.reshape(128, 16 * 128), "pw": pw.reshape(128, 4 * 128),
        "pvec": pv, "gbc": gbc.reshape(128, 3 * D), "ident": np.eye(128, dtype=np.float32),
    }
    return shared


def run(inputs, n_cores=8):
    xp = np.asarray(inputs["x_prompt"], dtype=np.float32)
    xs = np.asarray(inputs["x_sample"], dtype=np.float32)
    S = xp.shape[1]
    assert xs.shape[1] == S
    allx = np.concatenate([xp, xs], axis=0)
    ntot = allx.shape[0]
    assert ntot % n_cores == 0
    nseq = ntot // n_cores
    shared = host_prep(inputs, ntot)
    nc = build(nseq, S)
    in_maps = []
    for c in range(n_cores):
        m = dict(shared)
        m["x"] = np.ascontiguousarray(allx[c * nseq:(c + 1) * nseq].reshape(nseq * S, D))
        in_maps.append(m)
    res = run_bass_kernel_spmd(nc, in_maps, core_ids=list(range(n_cores)))
    outs = [np.asarray(r["y"], dtype=np.float32).reshape(nseq, S, D) for r in res.results]
    ally = np.concatenate(outs, axis=0)
    return ally[:xp.shape[0]], ally[xp.shape[0]:]


def kernel(**inputs):
    yp, ys = run(inputs, 8)
    return (np.ascontiguousarray(yp), np.ascontiguousarray(ys))
```

```python
import numpy as np
from contextlib import ExitStack
import concourse.bass as bass
import concourse.mybir as mybir
from concourse.bass_utils import run_bass_kernel_spmd

F32 = mybir.dt.float32
BF16 = mybir.dt.bfloat16
AF = mybir.ActivationFunctionType
ALU = mybir.AluOpType
AX = mybir.AxisListType

D = 1024
DFF = 2816
NF = DFF // 128
NKD = D // 128
TT = 512
EPS = 1e-6
NPV = 96
WINS = (2, 4, 8, 16)
ARENA_WORDS = 53200
SCHED_WINDOW = 700
SCHED_FFN = True
TBL_BIAS = 1.0
BAN_POOL = False
ALT_ENGINES = ("DVE", "POOL")


class Chan:
    __slots__ = ("sem", "count")

    def __init__(self, sem):
        self.sem = sem
        self.count = 0


class Eng:
    def __init__(self, e, ch):
        self.e = e
        self.ch = ch
        self.seen = {}


class Res:
    __slots__ = ("lw", "rd")

    def __init__(self):
        self.lw = None
        self.rd = {}


def RL(n):
    return [Res() for _ in range(n)]


class KB:
    def __init__(self, nc, es):
        self.nc = nc
        self.es = es
        self.chans = []
        self.rec = None
        self.PE = Eng(nc.tensor, self.chan("c_pe"))
        self.ACT = Eng(nc.scalar, self.chan("c_act"))
        self.DVE = Eng(nc.vector, self.chan("c_dve"))
        self.POOL = Eng(nc.gpsimd, self.chan("c_pool"))
        self.SP = Eng(nc.sync, self.chan("c_sp"))
        self.engs = [self.PE, self.ACT, self.DVE, self.POOL, self.SP]
        self.alt_engines = [e for e, nm in ((self.DVE, "DVE"), (self.POOL, "POOL"), (self.ACT, "ACT")) if nm in ALT_ENGINES]

    def chan(self, name):
        c = Chan(self.es.enter_context(self.nc.semaphore(name)))
        self.chans.append(c)
        return c

    def _sync(self, eng, reads, writes):
        deps = {}
        for r in reads:
            if r.lw is not None:
                ch, c = r.lw
                if deps.get(ch, 0) < c:
                    deps[ch] = c
        for w in writes:
            if w.lw is not None:
                ch, c = w.lw
                if deps.get(ch, 0) < c:
                    deps[ch] = c
            for ch, c in w.rd.items():
                if deps.get(ch, 0) < c:
                    deps[ch] = c
        for ch, c in deps.items():
            if eng.seen.get(ch, 0) < c:
                eng.e.wait_ge(ch.sem, c)
                eng.seen[ch] = c

    def op(self, eng, fn, reads=(), writes=(), dur=None, n=512, cls=None, tbl=None, alts=None):
        if self.rec is not None:
            if dur is None:
                dur = self.est(eng, n, cls)
            al = [(eng, fn, dur)]
            for (e2, f2) in (alts or ()):
                if e2 in self.alt_engines:
                    al.append((e2, f2, self.est(e2, n, cls)))
            if BAN_POOL and eng is self.POOL and len(al) > 1:
                al = al[1:]
                eng, fn, dur = al[0]
            self.rec.append(dict(kind="op", eng=eng, fn=fn, reads=list(reads), writes=list(writes), dur=dur,
                                 occ=dur, tbl=tbl, alts=al))
            return
        self._sync(eng, reads, writes)
        ins = fn()
        ch = eng.ch
        ch.count += 1
        ins.then_inc(ch.sem, 1)
        c = ch.count
        for r in reads:
            r.rd[ch] = c
        for w in writes:
            w.lw = (ch, c)
            w.rd = {}

    def dma(self, qeng, chan, out, in_, reads=(), writes=(), nbytes=524288, **kw):
        if self.rec is not None:
            self.rec.append(dict(kind="dma", eng=qeng, chan=chan, out=out, in_=in_, reads=list(reads),
                                 writes=list(writes), kw=kw, dur=2.5 + nbytes / 150e3, occ=0.15, tbl=None))
            return
        self._sync(qeng, reads, writes)
        ins = qeng.e.dma_start(out=out, in_=in_, **kw)
        chan.count += 16
        ins.then_inc(chan.sem, 16)
        c = chan.count
        for r in reads:
            r.rd[chan] = c
        for w in writes:
            w.lw = (chan, c)
            w.rd = {}

    def est(self, eng, n, cls):
        if eng is self.ACT:
            return 0.30 + n / 1200.0
        if eng is self.DVE:
            if cls == "scan":
                return 0.2 + 2.8 * n / 960.0
            if cls == "fast":
                return 0.16 + n / 1920.0
            return 0.2 + 1.5 * n / 960.0
        if eng is self.POOL:
            return 0.3 + 2.6 * n / 1000.0
        if eng is self.PE:
            return 0.22
        return 0.2

    def begin_record(self):
        self.rec = []

    def flush(self, window=700, verbose=False):
        rec = self.rec
        self.rec = None
        n = len(rec)
        if n == 0:
            return
        lastw = {}
        readers = {}
        preds = [set() for _ in range(n)]
        for i, o in enumerate(rec):
            for r in o["reads"]:
                k = id(r)
                if k in lastw:
                    preds[i].add(lastw[k])
            for w in o["writes"]:
                k = id(w)
                if k in lastw:
                    preds[i].add(lastw[k])
                for j in readers.get(k, ()):
                    preds[i].add(j)
            for r in o["reads"]:
                readers.setdefault(id(r), []).append(i)
            for w in o["writes"]:
                k = id(w)
                lastw[k] = i
                readers[k] = []
            preds[i].discard(i)
        succs = [[] for _ in range(n)]
        for i in range(n):
            for j in preds[i]:
                succs[j].append(i)
        prio = [0.0] * n
        for i in range(n - 1, -1, -1):
            m = 0.0
            for j in succs[i]:
                if prio[j] > m:
                    m = prio[j]
            prio[i] = m + rec[i]["dur"]
        npred = [len(p) for p in preds]
        finish = [0.0] * n
        free = {}
        cand = set(i for i in range(n) if npred[i] == 0)
        done = [False] * n
        lo = 0
        order = []
        act_tbl = [None]
        LAT = 0.15
        while len(order) < n:
            while lo < n and done[lo]:
                lo += 1
            best = None
            bkey = None
            for i in cand:
                if i >= lo + window:
                    continue
                o = rec[i]
                rdy = 0.0
                for j in preds[i]:
                    f = finish[j] + LAT
                    if f > rdy:
                        rdy = f
                if o["kind"] == "dma":
                    e = o["eng"]
                    st = max(free.get(e, 0.0), rdy)
                    key = (st, -prio[i], i)
                    if bkey is None or key < bkey:
                        bkey = key
                        best = (i, st, 0.0, None)
                    continue
                bfin = None
                for ai, (e, f_, du) in enumerate(o["alts"]):
                    st = max(free.get(e, 0.0), rdy)
                    pen = 0.0
                    if o["tbl"] is not None and e is self.ACT and act_tbl[0] != o["tbl"]:
                        pen = 1.3
                    fin = st + pen + du
                    if bfin is None or fin < bfin[0]:
                        bfin = (fin, st, pen, ai)
                key = (bfin[1] + bfin[2] * TBL_BIAS, -prio[i], i)
                if bkey is None or key < bkey:
                    bkey = key
                    best = (i, bfin[1], bfin[2], bfin[3])
            i, st, pen, ai = best
            o = rec[i]
            if ai is not None:
                e, f_, du = o["alts"][ai]
                o["eng"], o["fn"], o["dur"], o["occ"] = e, f_, du, du
            e = o["eng"]
            if o["tbl"] is not None and e is self.ACT:
                if act_tbl[0] != o["tbl"]:
                    self.nswitch = getattr(self, "nswitch", 0) + 1
                act_tbl[0] = o["tbl"]
            st += pen
            free[e] = st + o["occ"]
            finish[i] = st + o["dur"]
            done[i] = True
            cand.discard(i)
            order.append(i)
            for j in succs[i]:
                npred[j] -= 1
                if npred[j] == 0:
                    cand.add(j)
        self.last_makespan = max(finish)
        if verbose:
            busy = {}
            for o in rec:
                busy[o["eng"]] = busy.get(o["eng"], 0.0) + o["occ"]
            print("[sched] ops=%d makespan=%.1fus critpath=%.1fus tblsw=%d busy=%s" % (
                n, self.last_makespan, max(prio), getattr(self, "nswitch", 0), {("PE", "ACT", "DVE", "POOL", "SP")[self.engs.index(e)]: round(b, 1)
                                        for e, b in busy.items()}))
        for i in order:
            o = rec[i]
            if o["kind"] == "op":
                self.op(o["eng"], o["fn"], o["reads"], o["writes"])
            else:
                self.dma(o["eng"], o["chan"], o["out"], o["in_"], o["reads"], o["writes"], **o["kw"])

    def barrier(self):
        for eng in self.engs:
            for ch in self.chans:
                if ch.count > eng.seen.get(ch, 0):
                    eng.e.wait_ge(ch.sem, ch.count)
                    eng.seen[ch] = ch.count


class Arena:
    def __init__(self, ap, nwords):
        self.ap = ap
        self.n = nwords
        self.off = 0

    def view(self, off, shape, dtype):
        n = int(np.prod(shape))
        words = n if dtype == F32 else (n + 1) // 2
        assert off + words <= self.n, (off, words, self.n)
        v = self.ap[:, off:off + words]
        if dtype != F32:
            v = v.bitcast(dtype)
        if len(shape) == 2:
            v = v.rearrange("p (a b) -> p a b", a=shape[0])
        return v, words

    def alloc(self, shape, dtype=F32):
        v, words = self.view(self.off, shape, dtype)
        self.off += words
        return v


def build(nseq, S):
    NT = S // TT
    ntok = nseq * S
    nc = bass.Bass("TRN2", target_bir_lowering=False)

    def din(name, shape):
        return nc.dram_tensor(name, shape, F32, kind="ExternalInput").ap()

    x_d = din("x", [ntok, D])
    y_d = nc.dram_tensor("y", [ntok, D], F32, kind="ExternalOutput").ap()
    s1_d = nc.dram_tensor("s1", [ntok, D], F32, kind="Internal").ap()
    s2_d = nc.dram_tensor("s2", [ntok, D], F32, kind="Internal").ap()
    ht_d = nc.dram_tensor("ht", [NKD, 128, ntok], BF16, kind="Internal").ap()
    hb_d = nc.dram_tensor("hb", [4, 128, ntok], F32, kind="Internal").ap()
    xc_d = nc.dram_tensor("xc", [4, 128, ntok], F32, kind="Internal").ap()
    w1a_d = din("w1a", [D, 2 * DFF])
    w1b_d = din("w1b", [DFF, D])
    w2a_d = din("w2a", [D, 2 * DFF])
    w2b_d = din("w2b", [DFF, D])
    win_d = din("win", [D, 1536])
    wout_d = din("wout", [D, D])
    gw_d = din("gw", [128, 16 * 128])
    pw_d = din("pw", [128, 4 * 128])
    pvec_d = din("pvec", [128, NPV])
    gbc_d = din("gbc", [128, 3 * D])
    ident_d = din("ident", [128, 128])

    with ExitStack() as es:
        arena_t = es.enter_context(nc.sbuf_tensor("arena", [128, ARENA_WORDS], F32))
        banks = [es.enter_context(nc.psum_tensor("bank%d" % i, [128, 512], F32)) for i in range(8)]
        K = KB(nc, es)
        PE, ACT, DVE, POOL, SP = K.PE, K.ACT, K.DVE, K.POOL, K.SP
        ar = Arena(arena_t[:, :], ARENA_WORDS)
        P = [b[:, :] for b in banks]
        rP = RL(8)

        PV = ar.alloc((NPV,), F32)
        IDB = ar.alloc((128,), BF16)
        ONEB = ar.alloc((2,), BF16)
        persist_off = ar.off
        rPV, rID, rONE = Res(), Res(), Res()
        ch_c = K.chan("ch_const")
        K.dma(SP, ch_c, PV, pvec_d, writes=[rPV])
        ch_c2 = K.chan("ch_const2")
        K.dma(POOL, ch_c2, IDB, ident_d, writes=[rID])
        K.op(DVE, lambda: nc.vector.memset(ONEB, 1.0), writes=[rONE])
        K.op(DVE, lambda: nc.vector.memset(PV[:, 92:93], EPS), reads=[], writes=[rPV])
        K.op(DVE, lambda: nc.vector.memset(PV[:, 93:94], 1.0), reads=[], writes=[rPV])
        K.op(ACT, lambda: nc.scalar.activation(out=PV[:, 80:88], in_=PV[:, 60:68], func=AF.Exp, scale=-1.0),
             reads=[rPV], writes=[rPV])
        K.op(ACT, lambda: nc.scalar.activation(out=PV[:, 80:88], in_=PV[:, 80:88], func=AF.Ln,
                                               bias=PV[:, 93:94], scale=1.0), reads=[rPV], writes=[rPV])
        K.op(DVE, lambda: nc.vector.tensor_scalar(out=PV[:, 80:88], in0=PV[:, 80:88], scalar1=-8.0,
                                                  scalar2=None, op0=ALU.mult), reads=[rPV], writes=[rPV])
        K.op(DVE, lambda: nc.vector.tensor_tensor(out=PV[:, 88:92], in0=PV[:, 68:72], in1=PV[:, 76:80],
                                                  op=ALU.mult), reads=[rPV], writes=[rPV])
        K.barrier()
        EPSC = PV[:, 92:93]
        ONEC = PV[:, 93:94]

        def rows(dram, tok0, n=128):
            return dram[tok0:tok0 + n, :]

        def norm_block(Xb, rXb, Hb, rHb, ST, rst, b):
            K.op(ACT, lambda: nc.scalar.activation(out=Hb, in_=Xb, func=AF.Square, accum_out=ST[:, b:b + 1]),
                 reads=[rXb], writes=[rst[b], rHb], n=1024)
            K.op(ACT, lambda: nc.scalar.activation(out=ST[:, 4 + b:5 + b], in_=ST[:, b:b + 1], func=AF.Sqrt,
                                                   bias=EPSC, scale=1.0 / D),
                 reads=[rst[b]], writes=[rst[4 + b]], tbl="sqrt", dur=0.31)
            K.op(DVE, lambda: nc.vector.reciprocal(out=ST[:, 8 + b:9 + b], in_=ST[:, 4 + b:5 + b]),
                 reads=[rst[4 + b]], writes=[rst[8 + b]], dur=0.2)
            K.op(DVE, lambda: nc.vector.tensor_scalar(out=Hb, in0=Xb, scalar1=ST[:, 8 + b:9 + b],
                                                      scalar2=None, op0=ALU.mult),
                 reads=[rXb, rst[8 + b]], writes=[rHb], n=1024, cls="fast")

        def ffn_phase(src, dst, wa_d, wb_d, pgcol, gbi, tag, emit_ht):
            ar.off = persist_off
            W1 = ar.alloc((NKD, 2 * DFF), BF16)
            W2 = ar.alloc((NF, D), BF16)
            GB = ar.alloc((D,), F32)
            XS = [ar.alloc((D,), F32) for _ in range(2)]
            H = ar.alloc((4, D), BF16)
            HT = ar.alloc((NKD, TT), BF16)
            actb_off = ar.off
            ACTB = ar.alloc((NF, TT), BF16)
            SG = [ar.alloc((TT,), F32) for _ in range(2)]
            yb_off = ar.off
            YB = [ar.alloc((D,), F32) for _ in range(2)]
            xr_off = ar.off
            XR = [ar.alloc((D,), F32) for _ in range(2)]
            ST = ar.alloc((16,), F32)
            ST2 = ar.alloc((16,), F32)
            ST4 = ar.alloc((16,), F32)
            H2 = ar.alloc((D,), BF16)
            HTB = ar.alloc((NKD, 128), BF16)
            STG = [ar.view(actb_off + h * DFF, (DFF,), F32)[0] for h in range(2)]
            rW1 = RL(16)
            rW2, rGB, rH2, rHTB = Res(), Res(), Res(), Res()
            rXS, rH, rHT, rACT = RL(2), RL(4), RL(NKD), RL(NF)
            rSG, rXR, rSTG = RL(2), RL(2), RL(2)
            rYB = [RL(2), RL(2)]
            rst, rst2, rst4 = RL(16), RL(16), RL(16)
            chXS = [K.chan("%s_xs%d" % (tag, i)) for i in range(2)]
            chXR = [K.chan("%s_xr%d" % (tag, i)) for i in range(2)]
            chST = [K.chan("%s_st%d" % (tag, i)) for i in range(2)]
            chW = K.chan("%s_w" % tag)
            chGB = K.chan("%s_gb" % tag)
            chHT = K.chan("%s_ht" % tag)
            chSTG = [K.chan("%s_stg%d" % (tag, i)) for i in range(2)]
            PT = [P[6].bitcast(BF16)[:, 0:512], P[7].bitcast(BF16)[:, 0:512]]
            PTX = [P[6].bitcast(BF16), P[7].bitcast(BF16)]
            rPT = [rP[6], rP[7]]
            G, U, Y = [P[0], P[1]], [P[2], P[3]], [P[4], P[5]]
            rG, rU, rY = [rP[0], rP[1]], [rP[2], rP[3]], [rP[4], rP[5]]

            if SCHED_FFN:
                K.begin_record()
            NJB = NF // 2
            rW1p = [RL(NJB), RL(NJB)]
            STGV = [ar.view(yb_off, (NKD, 256), F32)[0], ar.view(xr_off, (NKD, 256), F32)[0]]
            rSTGV = [rYB[0] + rYB[1], list(rXR)]
            K.dma(SP, chGB, GB, gbc_d[:, gbi * D:(gbi + 1) * D], writes=[rGB], nbytes=524288)
            K.op(DVE, lambda: nc.vector.tensor_scalar(out=GB, in0=GB, scalar1=0.5, scalar2=None, op0=ALU.mult),
                 reads=[rGB], writes=[rGB], n=1024, cls="fast")
            pi = 0
            for jb in range(NJB):
                for h in range(2):
                    sl = pi % 2
                    col = h * DFF + jb * 256
                    K.dma(SP, chSTG[sl], STGV[sl], wa_d[:, col:col + 256].rearrange("(k p) c -> p k c", p=128),
                          writes=list(rSTGV[sl]), nbytes=1048576)
                    for k in range(NKD):
                        if k % 2 == 0:
                            K.op(DVE, lambda k=k, sl=sl, col=col: nc.vector.tensor_scalar(
                                out=W1[:, k, col:col + 256], in0=STGV[sl][:, k, :],
                                scalar1=PV[:, pgcol + k:pgcol + k + 1], scalar2=None, op0=ALU.mult),
                                reads=list(rSTGV[sl]) + [rPV], writes=[rW1p[h][jb]], n=256, cls="fast")
                        else:
                            K.op(ACT, lambda k=k, sl=sl, col=col: nc.scalar.activation(
                                out=W1[:, k, col:col + 256], in_=STGV[sl][:, k, :], func=AF.Copy,
                                scale=PV[:, pgcol + k:pgcol + k + 1]),
                                reads=list(rSTGV[sl]) + [rPV], writes=[rW1p[h][jb]], n=256)
                    pi += 1
            for f0 in range(0, NF, 2):
                K.dma(POOL, chW, W2[:, f0:f0 + 2, :],
                      wb_d[f0 * 128:(f0 + 2) * 128, :].rearrange("(f p) d -> p f d", p=128),
                      reads=[rW1p[1][NJB // 2]], writes=[rW2], nbytes=1048576)

            ntile = ntok // TT

            def do_pre_block(t, b):
                slot = b % 2
                K.dma(SP, chXS[slot], XS[slot], rows(src, t * TT + b * 128), writes=[rXS[slot]])
                norm_block(XS[slot], rXS[slot], H[:, b, :], rH[b], ST, rst, b)

            def do_T(t):
                for k in range(NKD):
                    pt = PT[k % 2]

                    def f(k=k, pt=pt):
                        ins = None
                        for b in range(4):
                            ins = nc.tensor.transpose(out=pt[:, b * 128:(b + 1) * 128],
                                                      in_=H[:, b, k * 128:(k + 1) * 128], identity=IDB)
                        return ins
                    K.op(PE, f, reads=list(rH) + [rID], writes=[rPT[k % 2]], dur=0.5)
                    if k % 2 == 0:
                        K.op(ACT, lambda k=k, pt=pt: nc.scalar.copy(out=HT[:, k, :], in_=pt),
                             reads=[rPT[k % 2]], writes=[rHT[k]])
                    else:
                        K.op(DVE, lambda k=k, pt=pt: nc.vector.tensor_copy(out=HT[:, k, :], in_=pt),
                             reads=[rPT[k % 2]], writes=[rHT[k]])

            pending = []

            def emit_ht_T():
                tok0, par = pending.pop(0)
                ptx = PTX[par]

                def f():
                    ins = None
                    for k in range(NKD):
                        ins = nc.tensor.transpose(out=ptx[:, k * 128:(k + 1) * 128],
                                                  in_=H2[:, k * 128:(k + 1) * 128], identity=IDB)
                    return ins
                K.op(PE, f, reads=[rH2, rID], writes=[rPT[par]], dur=0.9)
                K.op(DVE, lambda: nc.vector.tensor_copy(out=HTB, in_=ptx.rearrange("p (k n) -> p k n", k=NKD)),
                     reads=[rPT[par]], writes=[rHTB], n=1024, cls="fast")
                K.dma(SP, chHT, ht_d[:, :, tok0:tok0 + 128].rearrange("k p n -> p k n"), HTB, reads=[rHTB])

            for b in range(4):
                do_pre_block(0, b)
            do_T(0)
            blkctr = 0

            for t in range(ntile):
                pre_at = {3: 0, 8: 1, 13: 2, 18: 3} if t + 1 < ntile else {}
                for j in range(NF):
                    def fg(j=j, col=j * 128, bank=G[j % 2]):
                        ins = None
                        for k in range(NKD):
                            ins = nc.tensor.matmul(bank, lhsT=W1[:, k, col:col + 128], rhs=HT[:, k, :],
                                                   start=(k == 0), stop=(k == NKD - 1))
                        return ins
                    K.op(PE, fg, reads=rHT + [rW1p[0][j // 2]], writes=[rG[j % 2]], dur=1.75)

                    def fu(j=j, col=DFF + j * 128, bank=U[j % 2]):
                        ins = None
                        for k in range(NKD):
                            ins = nc.tensor.matmul(bank, lhsT=W1[:, k, col:col + 128], rhs=HT[:, k, :],
                                                   start=(k == 0), stop=(k == NKD - 1))
                        return ins
                    K.op(PE, fu, reads=rHT + [rW1p[1][j // 2]], writes=[rU[j % 2]], dur=1.75)
                    K.op(ACT, lambda j=j: nc.scalar.activation(out=SG[j % 2], in_=G[j % 2], func=AF.Silu),
                         reads=[rG[j % 2]], writes=[rSG[j % 2]], tbl="silu")
                    K.op(DVE, lambda j=j: nc.vector.tensor_tensor(out=ACTB[:, j, :], in0=U[j % 2], in1=SG[j % 2],
                                                                  op=ALU.mult),
                         reads=[rU[j % 2], rSG[j % 2]], writes=[rACT[j]])
                    if j == 1 and pending:
                        emit_ht_T()
                    if j in pre_at:
                        do_pre_block(t + 1, pre_at[j])
                if t + 1 < ntile:
                    do_T(t + 1)
                for b in range(2):
                    K.dma(SP, chXR[b], XR[b], rows(src, t * TT + b * 128), writes=[rXR[b]])
                for b in range(4):
                    slot = b % 2
                    for hf in range(2):
                        def fy(b=b, hf=hf):
                            ins = None
                            for f in range(NF):
                                ins = nc.tensor.matmul(Y[hf], lhsT=ACTB[:, f, b * 128:(b + 1) * 128],
                                                       rhs=W2[:, f, hf * 512:(hf + 1) * 512],
                                                       start=(f == 0), stop=(f == NF - 1))
                            return ins
                        K.op(PE, fy, reads=rACT + [rW2], writes=[rY[hf]], dur=4.8)
                        if hf == 0:
                            K.op(ACT, lambda slot=slot, hf=hf: nc.scalar.copy(
                                out=YB[slot][:, hf * 512:(hf + 1) * 512], in_=Y[hf]),
                                reads=[rY[hf]], writes=[rYB[slot][hf]])
                        else:
                            K.op(DVE, lambda slot=slot, hf=hf: nc.vector.tensor_copy(
                                out=YB[slot][:, hf * 512:(hf + 1) * 512], in_=Y[hf]),
                                reads=[rY[hf]], writes=[rYB[slot][hf]])
                    if pending:
                        emit_ht_T()
                    K.op(ACT, lambda slot=slot, b=b: nc.scalar.activation(
                        out=SG[0].bitcast(BF16), in_=YB[slot], func=AF.Square, accum_out=ST2[:, b:b + 1]),
                        reads=rYB[slot], writes=[rst2[b], rSG[0]], n=1024)
                    K.op(ACT, lambda b=b: nc.scalar.activation(out=ST2[:, 4 + b:5 + b], in_=ST2[:, b:b + 1],
                                                               func=AF.Sqrt, bias=EPSC, scale=1.0 / D),
                         reads=[rst2[b]], writes=[rst2[4 + b]], tbl="sqrt", dur=0.31)
                    K.op(DVE, lambda b=b: nc.vector.reciprocal(out=ST2[:, 8 + b:9 + b], in_=ST2[:, 4 + b:5 + b]),
                         reads=[rst2[4 + b]], writes=[rst2[8 + b]], dur=0.2)
                    K.op(DVE, lambda slot=slot, b=b: nc.vector.scalar_tensor_tensor(
                        out=YB[slot], in0=YB[slot], scalar=ST2[:, 8 + b:9 + b], in1=GB,
                        op0=ALU.mult, op1=ALU.mult),
                        reads=rYB[slot] + [rst2[8 + b], rGB], writes=rYB[slot], n=1024)
                    K.op(POOL, lambda slot=slot: nc.gpsimd.tensor_tensor(out=XR[slot], in0=YB[slot], in1=XR[slot],
                                                                         op=ALU.add),
                         reads=rYB[slot] + [rXR[slot]], writes=[rXR[slot]], n=1024)
                    tok0 = t * TT + b * 128
                    K.dma(SP, chST[slot], rows(dst, tok0), XR[slot], reads=[rXR[slot]])
                    if emit_ht:
                        norm_block(XR[slot], rXR[slot], H2, rH2, ST4, rst4, b)
                        pending.append((tok0, blkctr % 2))
                        blkctr += 1
                    if b + 2 < 4:
                        K.dma(SP, chXR[slot], XR[slot], rows(src, t * TT + (b + 2) * 128), writes=[rXR[slot]])
            while pending:
                emit_ht_T()
            if SCHED_FFN:
                K.flush(window=SCHED_WINDOW, verbose=True)
            K.barrier()

        def mixer_phase(src, dst):
            ar.off = persist_off
            WIN = ar.alloc((NKD, 1536), BF16)
            WOUT = ar.alloc((NKD, D), BF16)
            GW = ar.alloc((16, 128), BF16)
            PWT = ar.alloc((4, 128), BF16)
            GBm = ar.alloc((D,), F32)
            CB = ar.alloc((NT, 4), F32)
            CF = ar.alloc((4,), F32)
            CBR = ar.alloc((4,), F32)
            sub_off = ar.off
            HW = [ar.alloc((NKD, 528), BF16) for _ in range(2)]
            ZX = ar.alloc((4, 528), F32)
            ZP = ar.alloc((4, 528), F32)
            GEL = ar.alloc((4, TT), F32)
            XC = ar.alloc((4, TT), F32)
            XCB = ar.alloc((4, TT), BF16)
            ta_off = ar.off
            TA = [[ar.alloc((TT,), F32) for _ in range(3)] for _ in range(8)]
            HF = ar.alloc((4, TT), F32)
            VB = ar.alloc((4, TT), BF16)
            PB = ar.alloc((4, TT), BF16)
            SQ = [ar.alloc((TT,), BF16) for _ in range(4)]
            PA = [ar.alloc((528,), F32) for _ in range(2)]
            DD = ar.alloc((4, TT), BF16)
            O = [ar.alloc((D,), F32) for _ in range(2)]
            XR = [ar.alloc((D,), F32) for _ in range(2)]
            ST3 = ar.alloc((40,), F32)
            rWIN, rWOUT, rGW, rPWT, rGBm = Res(), Res(), Res(), Res(), Res()
            rCB, rCF, rCBR = RL(NT), RL(4), RL(4)
            rHW = RL(2)
            rZX, rZP, rGEL, rXC, rXCB = RL(4), RL(4), RL(4), RL(4), RL(4)
            rTA = [RL(3) for _ in range(8)]
            rHF, rVB, rPB, rDD = RL(4), RL(4), RL(4), RL(4)
            rSQ, rPA, rXR = RL(4), RL(2), RL(2)
            rO = [RL(2), RL(2)]
            rst3 = RL(40)

            def ta_view(j0):
                v, _ = ar.view(ta_off + j0 * TT, (4, TT), F32)
                return v, [rTA[j // 3][j % 3] for j in range(j0, j0 + 4)]
            HBS0, rHBS0 = ta_view(12)
            HBS1, rHBS1 = ta_view(16)
            XC2, rXC2 = ta_view(20)
            HBS, rHBS = [HBS0, HBS1], [rHBS0, rHBS1]
            XCS, rXCS = [XC, XC2], [rXC, rXC2]
            chXCS = [K.chan("mx_xcs%d" % i) for i in range(2)]
            chHBS = [K.chan("mx_hbs%d" % i) for i in range(8)]
            chXCL = [K.chan("mx_xcl%d" % i) for i in range(2)]
            chHBL = [K.chan("mx_hbl%d" % i) for i in range(2)]
            chW = K.chan("mx_w")
            chSTG = K.chan("mx_stg")
            chGBm = K.chan("mx_gb")
            chHW = [K.chan("mx_hw%d" % i) for i in range(2)]
            chXR = [K.chan("mx_xr%d" % i) for i in range(2)]
            chST = [K.chan("mx_st%d" % i) for i in range(2)]
            STG, _ = ar.view(sub_off, (1536,), F32)
            rSTG = Res()
            for k in range(NKD):
                K.dma(SP, chSTG, STG, win_d[k * 128:(k + 1) * 128, :], writes=[rSTG])
                K.op(DVE, lambda k=k: nc.vector.tensor_scalar(out=WIN[:, k, :], in0=STG,
                                                              scalar1=PV[:, 8 + k:9 + k], scalar2=None,
                                                              op0=ALU.mult),
                     reads=[rSTG, rPV], writes=[rWIN])
            for k in range(4):
                K.dma(SP, chSTG, STG[:, 0:D], wout_d[k * 128:(k + 1) * 128, :], writes=[rSTG])
                K.op(DVE, lambda k=k: nc.vector.tensor_scalar(out=WOUT[:, k, :], in0=STG[:, 0:D],
                                                              scalar1=PV[:, 72 + k:73 + k], scalar2=None,
                                                              op0=ALU.mult),
                     reads=[rSTG, rPV], writes=[rWOUT])
            for k0 in range(4, NKD, 2):
                K.dma(POOL, chW, WOUT[:, k0:k0 + 2, :],
                      wout_d[k0 * 128:(k0 + 2) * 128, :].rearrange("(f p) d -> p f d", p=128), writes=[rWOUT])
            K.dma(POOL, chW, GW, gw_d.rearrange("p (a b) -> p a b", a=16), writes=[rGW])
            K.dma(POOL, chW, PWT, pw_d.rearrange("p (a b) -> p a b", a=4), writes=[rPWT])
            K.dma(SP, chGBm, GBm, gbc_d[:, D:2 * D], writes=[rGBm])
            K.barrier()

            ZM, rZM = [P[0], P[1]], [rP[0], rP[1]]
            ZH, rZH = P[2], rP[2]
            GPS = [([P[3], P[4]], [rP[3], rP[4]]), ([P[5], P[6]], [rP[5], rP[6]])]
            OL, rOL = P[5], rP[5]
            OP, rOP = P[6], rP[6]
            SSB, rSSB = P[7], rP[7]
            state = {"hw": 0, "zi": 0, "sq": 0, "b2": 0}

            def load_window(tok_s, t):
                slot = state["hw"] % 2
                state["hw"] += 1
                hw = HW[slot]
                c0 = tok_s + t * TT
                lo = 0 if t == 0 else 8
                hi = 0 if t == NT - 1 else 8
                K.dma(SP, chHW[slot], hw[:, :, 8 - lo:520 + hi],
                      ht_d[:, :, c0 - lo:c0 + TT + hi].rearrange("k p n -> p k n"), writes=[rHW[slot]], nbytes=1081344)
                if lo == 0:
                    K.op(POOL, lambda: nc.gpsimd.memset(hw[:, :, 0:8], 0.0), writes=[rHW[slot]], dur=0.3)
                if hi == 0:
                    K.op(POOL, lambda: nc.gpsimd.memset(hw[:, :, 520:528], 0.0), writes=[rHW[slot]], dur=0.3)
                return slot

            def inproj_chunk(slot, oc, halo):
                hw = HW[slot]
                zi = state["zi"]
                state["zi"] += 1
                zm, rzm = ZM[zi % 2], rZM[zi % 2]

                def fm():
                    ins = None
                    for k in range(NKD):
                        ins = nc.tensor.matmul(zm, lhsT=WIN[:, k, oc * 128:(oc + 1) * 128],
                                               rhs=hw[:, k, 8:520], start=(k == 0), stop=(k == NKD - 1))
                    return ins
                K.op(PE, fm, reads=[rHW[slot], rWIN], writes=[rzm], dur=1.76)
                if halo:
                    for side in range(2):
                        hc = 0 if side == 0 else 520

                        def fh(side=side, hc=hc):
                            ins = None
                            for k in range(NKD):
                                ins = nc.tensor.matmul(
                                    ZH[:, oc * 16 + side * 8:oc * 16 + side * 8 + 8],
                                    lhsT=WIN[:, k, oc * 128:(oc + 1) * 128], rhs=hw[:, k, hc:hc + 8],
                                    start=(k == 0), stop=(k == NKD - 1))
                            return ins
                        K.op(PE, fh, reads=[rHW[slot], rWIN], writes=[rZH], dur=0.56)
                return zm, rzm

            def evac_window(Z, rZc, c, oc, zm, rzm, main_eng):
                if main_eng is ACT:
                    K.op(ACT, lambda: nc.scalar.copy(out=Z[:, c, 8:520], in_=zm), reads=[rzm], writes=[rZc[c]])
                else:
                    K.op(DVE, lambda: nc.vector.tensor_copy(out=Z[:, c, 8:520], in_=zm), reads=[rzm],
                         writes=[rZc[c]])
                K.op(DVE, lambda: nc.vector.tensor_copy(out=Z[:, c, 0:8], in_=ZH[:, oc * 16:oc * 16 + 8]),
                     reads=[rZH], writes=[rZc[c]])
                K.op(DVE, lambda: nc.vector.tensor_copy(out=Z[:, c, 520:528], in_=ZH[:, oc * 16 + 8:oc * 16 + 16]),
                     reads=[rZH], writes=[rZc[c]], dur=0.17)

            FS0 = (ZX, rZX, XC, rXC, XCB, rXCB)
            FS1 = (ZP, rZP, GEL, rGEL, VB, rVB)

            def conv_chunk(c, fs):
                Zf, rZf, XCf, rXCf, XCBf, rXCBf = fs
                K.op(ACT, lambda: nc.scalar.activation(
                    out=XCf[:, c, :], in_=Zf[:, c, 6:6 + TT], func=AF.Identity, scale=PV[:, 24 + c:25 + c],
                    bias=PV[:, 40 + c:41 + c]),
                    reads=[rZf[c], rPV], writes=[rXCf[c]],
                    alts=[(POOL, lambda: nc.gpsimd.tensor_scalar(
                        out=XCf[:, c, :], in0=Zf[:, c, 6:6 + TT], scalar1=PV[:, 24 + c:25 + c],
                        scalar2=PV[:, 40 + c:41 + c], op0=ALU.mult, op1=ALU.add))])
                for kk in range(1, 4):
                    K.op(DVE, lambda kk=kk: nc.vector.scalar_tensor_tensor(
                        out=XCf[:, c, :], in0=Zf[:, c, 6 + kk:6 + kk + TT],
                        scalar=PV[:, 24 + kk * 4 + c:25 + kk * 4 + c], in1=XCf[:, c, :],
                        op0=ALU.mult, op1=ALU.add),
                        reads=[rZf[c], rPV, rXCf[c]], writes=[rXCf[c]])
                K.op(ACT, lambda: nc.scalar.copy(out=XCBf[:, c, :], in_=XCf[:, c, :]),
                     reads=[rXCf[c]], writes=[rXCBf[c]],
                     alts=[(DVE, lambda: nc.vector.tensor_copy(out=XCBf[:, c, :], in_=XCf[:, c, :])),
                           (POOL, lambda: nc.gpsimd.tensor_copy(out=XCBf[:, c, :], in_=XCf[:, c, :]))])

            def lru_stage(k, job, i):
                dr, c = job["dr"], job["c"]
                Zf, rZf, XCf, rXCf, XCBf, rXCBf = job.get("fs", FS0)
                T1, T2, T3 = TA[i]
                r1, r2, r3 = rTA[i]
                gp, rgp = GPS[i % 2]
                if k == 0:
                    ga = GW[:, dr * 8 + c, :]
                    gx = GW[:, dr * 8 + 4 + c, :]
                    K.op(PE, lambda: nc.tensor.matmul(gp[0], lhsT=ga, rhs=XCBf[:, c, :], start=True, stop=True),
                         reads=[rGW, rXCBf[c]], writes=[rgp[0]])
                    K.op(PE, lambda: nc.tensor.matmul(gp[1], lhsT=gx, rhs=XCBf[:, c, :], start=True, stop=True),
                         reads=[rGW, rXCBf[c]], writes=[rgp[1]])
                elif k == 1:
                    ba = PV[:, 44 + dr * 4 + c:45 + dr * 4 + c]
                    bx = PV[:, 52 + dr * 4 + c:53 + dr * 4 + c]
                    K.op(ACT, lambda: nc.scalar.activation(out=T1, in_=gp[0], func=AF.Sigmoid, bias=ba, scale=1.0),
                         reads=[rgp[0], rPV], writes=[r1], tbl="sig")
                    K.op(ACT, lambda: nc.scalar.activation(out=T2, in_=gp[1], func=AF.Sigmoid, bias=bx, scale=1.0),
                         reads=[rgp[1], rPV], writes=[r2], tbl="sig")
                elif k == 2:
                    nsp = PV[:, 80 + dr * 4 + c:81 + dr * 4 + c]
                    K.op(ACT, lambda: nc.scalar.activation(out=T3, in_=T1, func=AF.Exp, scale=nsp),
                         reads=[r1, rPV], writes=[r3], tbl="exp")
                    K.op(POOL, lambda: nc.gpsimd.tensor_tensor(out=T2, in0=T2, in1=XCf[:, c, :], op=ALU.mult),
                         reads=[r2, rXCf[c]], writes=[r2],
                         alts=[(DVE, lambda: nc.vector.tensor_tensor(out=T2, in0=T2, in1=XCf[:, c, :], op=ALU.mult))])
                elif k == 3:
                    K.op(ACT, lambda: nc.scalar.activation(out=T1, in_=T3, func=AF.Square),
                         reads=[r3, r1], writes=[r1],
                         alts=[(POOL, lambda: nc.gpsimd.tensor_tensor(out=T1, in0=T3, in1=T3, op=ALU.mult)),
                               (DVE, lambda: nc.vector.tensor_tensor(out=T1, in0=T3, in1=T3, op=ALU.mult))])
                elif k == 4:
                    K.op(ACT, lambda: nc.scalar.activation(out=T1, in_=T1, func=AF.Sqrt, bias=ONEC, scale=-1.0),
                         reads=[r1, rPV], writes=[r1], tbl="sqrt")
                elif k == 5:
                    K.op(DVE, lambda: nc.vector.tensor_tensor(out=T2, in0=T2, in1=T1, op=ALU.mult),
                         reads=[r1, r2], writes=[r2],
                         alts=[(POOL, lambda: nc.gpsimd.tensor_tensor(out=T2, in0=T2, in1=T1, op=ALU.mult))])
                elif k == 6:
                    out_ap, r_out = job["out"]
                    init_ap, r_init = job["init"]
                    if dr == 0:
                        K.op(DVE, lambda: nc.vector.tensor_tensor_scan(out=out_ap, data0=T3, data1=T2,
                                                                       initial=init_ap, op0=ALU.mult, op1=ALU.add),
                             reads=[r2, r3, r_init], writes=[r_out], cls="scan")
                    else:
                        K.op(DVE, lambda: nc.vector.tensor_tensor_scan(out=out_ap[:, ::-1], data0=T3[:, ::-1],
                                                                       data1=T2[:, ::-1], initial=init_ap,
                                                                       op0=ALU.mult, op1=ALU.add),
                             reads=[r2, r3, r_init], writes=[r_out], cls="scan")
                elif k == 7:
                    job["post"](job, i)

            def lru_A(jobs):
                for i, job in enumerate(jobs):
                    lru_stage(0, job, i)
                    lru_stage(1, job, i)
                for i, job in enumerate(jobs):
                    lru_stage(2, job, i)

            def lru_st(jobs, k0, k1):
                for k in range(k0, k1):
                    for i, job in enumerate(jobs):
                        lru_stage(k, job, i)

            def front(slot, fs=None):
                fs = fs or FS0
                for c in range(4):
                    zm, rzm = inproj_chunk(slot, c, True)
                    evac_window(fs[0], fs[1], c, c, zm, rzm, ACT)
                    conv_chunk(c, fs)

            K.begin_record()
            for s in range(nseq):
                tok_s = s * S
                rXCD = RL(NT)
                rHBD = [RL(4) for _ in range(NT)]
                for c in range(4):
                    K.op(POOL, lambda c=c: nc.gpsimd.memset(CBR[:, c:c + 1], 0.0), writes=[rCBR[c]], dur=0.3)
                b1_tiles = list(range(NT - 1, -1, -1))
                groups = [b1_tiles[i:i + 2] for i in range(0, len(b1_tiles), 2)]
                FSS = [FS0, FS1]
                for grp in groups:
                    slots = [load_window(tok_s, t) for t in grp]
                    for gi, t in enumerate(grp):
                        front(slots[gi], FSS[gi])
                        tok0 = tok_s + t * TT
                        K.dma(SP, chXCS[gi], xc_d[:, :, tok0:tok0 + TT].rearrange("c p n -> p c n"), FSS[gi][2],
                              reads=list(FSS[gi][3]), writes=[rXCD[t]], nbytes=1048576)
                    jobs = []
                    for gi, t in enumerate(grp):
                        for c in range(4):
                            i = gi * 4 + c
                            if gi == 0:
                                init = (CBR[:, c:c + 1], rCBR[c])
                            else:
                                init = (TA[c][0][:, 0:1], rTA[c][0])
                            last = (gi == len(grp) - 1)

                            def post_b1(job, i, t=t, last=last):
                                c = job["c"]
                                T1 = TA[i][0]
                                tok0 = tok_s + t * TT
                                if last:
                                    K.op(ACT, lambda: nc.scalar.copy(out=CBR[:, c:c + 1], in_=T1[:, 0:1]),
                                         reads=[rTA[i][0]], writes=[rCBR[c]], dur=0.31)
                                K.dma(SP, chHBS[i], hb_d[c, :, tok0:tok0 + TT], T1,
                                      reads=[rTA[i][0]], writes=[rHBD[t][c]], nbytes=262144)
                            jobs.append(dict(dr=1, c=c, init=init, out=(TA[i][0], rTA[i][0]), post=post_b1,
                                             fs=FSS[gi]))
                    lru_A(jobs)
                    lru_st(jobs, 3, 8)

                for c in range(4):
                    K.op(POOL, lambda c=c: nc.gpsimd.memset(CF[:, c:c + 1], 0.0), writes=[rCF[c]], dur=0.3)

                def load_b2(t):
                    sl = state["b2"] % 2
                    state["b2"] += 1
                    tok0 = tok_s + t * TT
                    K.dma(SP, chXCL[sl], XCS[sl], xc_d[:, :, tok0:tok0 + TT].rearrange("c p n -> p c n"),
                          reads=[rXCD[t]], writes=list(rXCS[sl]), nbytes=1048576)
                    K.dma(SP, chHBL[sl], HBS[sl], hb_d[:, :, tok0:tok0 + TT].rearrange("c p n -> p c n"),
                          reads=list(rHBD[t]), writes=list(rHBS[sl]), nbytes=1048576)
                    return sl
                slot = load_window(tok_s, 0)
                bsl = load_b2(0)
                pend = []
                for t in range(NT):
                    if t + 1 < NT:
                        nslot = load_window(tok_s, t + 1)
                        nbsl = load_b2(t + 1)
                    XCt, rXCt = XCS[bsl], rXCS[bsl]
                    HBt, rHBt = HBS[bsl], rHBS[bsl]
                    for c in range(4):
                        K.op(ACT, lambda c=c, XCt=XCt: nc.scalar.copy(out=XCB[:, c, :], in_=XCt[:, c, :]),
                             reads=[rXCt[c]], writes=[rXCB[c]],
                             alts=[(DVE, lambda c=c, XCt=XCt: nc.vector.tensor_copy(out=XCB[:, c, :], in_=XCt[:, c, :])),
                                   (POOL, lambda c=c, XCt=XCt: nc.gpsimd.tensor_copy(out=XCB[:, c, :],
                                                                                      in_=XCt[:, c, :]))])
                    for c in range(4):
                        zm, rzm = inproj_chunk(slot, 4 + c, False)
                        K.op(ACT, lambda c=c, zm=zm: nc.scalar.activation(out=GEL[:, c, :], in_=zm,
                                                                          func=AF.Gelu_apprx_tanh),
                             reads=[rzm], writes=[rGEL[c]], tbl="gelu")

                    def post_f(job, i, HBt=HBt, rHBt=rHBt):
                        c = job["c"]
                        K.op(ACT, lambda: nc.scalar.copy(out=CF[:, c:c + 1], in_=HF[:, c, TT - 1:TT]),
                             reads=[rHF[c]], writes=[rCF[c]], dur=0.31)
                        K.op(DVE, lambda: nc.vector.tensor_tensor(out=HF[:, c, :], in0=HF[:, c, :], in1=HBt[:, c, :],
                                                                  op=ALU.add),
                             reads=[rHF[c], rHBt[c]], writes=[rHF[c]],
                             alts=[(POOL, lambda: nc.gpsimd.tensor_tensor(out=HF[:, c, :], in0=HF[:, c, :],
                                                                          in1=HBt[:, c, :], op=ALU.add))])
                        K.op(POOL, lambda: nc.gpsimd.tensor_tensor(out=VB[:, c, :], in0=HF[:, c, :], in1=GEL[:, c, :],
                                                                   op=ALU.mult),
                             reads=[rHF[c], rGEL[c]], writes=[rVB[c]],
                             alts=[(DVE, lambda: nc.vector.tensor_tensor(out=VB[:, c, :], in0=HF[:, c, :],
                                                                         in1=GEL[:, c, :], op=ALU.mult))])
                        sqi = state["sq"] % 4
                        state["sq"] += 1
                        sq, rsq = SQ[sqi], rSQ[sqi]
                        K.op(ACT, lambda: nc.scalar.activation(out=sq, in_=VB[:, c, :], func=AF.Square),
                             reads=[rVB[c]], writes=[rsq],
                             alts=[(POOL, lambda: nc.gpsimd.tensor_tensor(out=sq, in0=VB[:, c, :], in1=VB[:, c, :],
                                                                          op=ALU.mult)),
                                   (DVE, lambda: nc.vector.tensor_tensor(out=sq, in0=VB[:, c, :], in1=VB[:, c, :],
                                                                         op=ALU.mult))])

                        def fss():
                            ins = None
                            for blk in range(4):
                                col = blk * 4 + c
                                ins = nc.tensor.matmul(SSB[:, col:col + 1], lhsT=sq[:, blk * 128:(blk + 1) * 128],
                                                       rhs=ONEB[:, 0:1], start=True, stop=True)
                            return ins
                        K.op(PE, fss, reads=[rsq, rONE], writes=[rSSB], dur=0.3)

                    fsb = (None, None, XCt, rXCt, XCB, rXCB)
                    jobs = []
                    for c in range(4):
                        jobs.append(dict(dr=0, c=c, init=(CF[:, c:c + 1], rCF[c]), out=(HF[:, c, :], rHF[c]),
                                         post=post_f, fs=fsb))
                    lru_A(jobs)
                    if pend:
                        pend.pop(0)()
                    for c in range(4):
                        zm, rzm = inproj_chunk(slot, 8 + c, True)
                        evac_window(ZP, rZP, c, 8 + c, zm, rzm, ACT)
                    lru_st(jobs, 3, 4)

                    for g in range(4):
                        w = WINS[g]
                        A, B = PA
                        rA, rB = rPA
                        K.op(POOL, lambda g=g: nc.gpsimd.tensor_tensor(out=A[:, 0:527], in0=ZP[:, g, 0:527],
                                                                       in1=ZP[:, g, 1:528], op=ALU.add),
                             reads=[rZP[g]], writes=[rA],
                             alts=[(DVE, lambda g=g: nc.vector.tensor_tensor(out=A[:, 0:527], in0=ZP[:, g, 0:527],
                                                                             in1=ZP[:, g, 1:528], op=ALU.add))])
                        cur, rcur, oth, roth = A, rA, B, rB
                        n = 527
                        step = 2
                        for lvl in range(g):
                            n2 = n - step
                            K.op(POOL, lambda cur=cur, oth=oth, n2=n2, step=step: nc.gpsimd.tensor_tensor(
                                out=oth[:, 0:n2], in0=cur[:, 0:n2], in1=cur[:, step:step + n2], op=ALU.add),
                                reads=[rcur], writes=[roth],
                                alts=[(DVE, lambda cur=cur, oth=oth, n2=n2, step=step: nc.vector.tensor_tensor(
                                    out=oth[:, 0:n2], in0=cur[:, 0:n2], in1=cur[:, step:step + n2], op=ALU.add))])
                            cur, rcur, oth, roth = oth, roth, cur, rcur
                            n = n2
                            step *= 2
                        off = 8 - w // 2
                        K.op(DVE, lambda g=g, cur=cur, off=off, w=w: nc.vector.scalar_tensor_tensor(
                            out=DD[:, g, :], in0=cur[:, off:off + TT], scalar=1.0 / w, in1=ZP[:, g, 8:8 + TT],
                            op0=ALU.mult, op1=ALU.subtract),
                            reads=[rcur, rZP[g]], writes=[rDD[g]])
                        fixes = []
                        if t == 0:
                            for tk in range(w // 2):
                                fixes.append((tk, tk + w // 2))
                        if t == NT - 1:
                            for m in range(1, w // 2):
                                fixes.append((TT - m, w // 2 + m))
                        for (col, cnt) in fixes:
                            K.op(DVE, lambda g=g, cur=cur, off=off, col=col, cnt=cnt: nc.vector.scalar_tensor_tensor(
                                out=DD[:, g, col:col + 1], in0=cur[:, off + col:off + col + 1], scalar=1.0 / cnt,
                                in1=ZP[:, g, 8 + col:9 + col], op0=ALU.mult, op1=ALU.subtract),
                                reads=[rcur, rZP[g], rDD[g]], writes=[rDD[g]], dur=0.17)
                        gp = ZM[g % 2]
                        rgp = rZM[g % 2]
                        K.op(PE, lambda g=g, gp=gp: nc.tensor.matmul(gp, lhsT=PWT[:, g, :], rhs=DD[:, g, :],
                                                                     start=True, stop=True),
                             reads=[rPWT, rDD[g]], writes=[rgp])
                        K.op(ACT, lambda g=g, gp=gp: nc.scalar.activation(out=PB[:, g, :], in_=gp, func=AF.Copy,
                                                                          scale=PV[:, 88 + g:89 + g]),
                             reads=[rgp, rPV], writes=[rPB[g]])
                        sqi = state["sq"] % 4
                        state["sq"] += 1
                        sq, rsq = SQ[sqi], rSQ[sqi]
                        K.op(ACT, lambda g=g, gp=gp, sq=sq: nc.scalar.activation(out=sq, in_=gp, func=AF.Square,
                                                                                 scale=PV[:, 68 + g:69 + g]),
                             reads=[rgp, rPV], writes=[rsq])

                        def fsp(g=g, sq=sq):
                            ins = None
                            for blk in range(4):
                                col = 16 + blk * 4 + g
                                ins = nc.tensor.matmul(SSB[:, col:col + 1], lhsT=sq[:, blk * 128:(blk + 1) * 128],
                                                       rhs=ONEB[:, 0:1], start=True, stop=True)
                            return ins
                        K.op(PE, fsp, reads=[rsq, rONE], writes=[rSSB], dur=0.3)
                    lru_st(jobs, 4, 8)
                    K.op(DVE, lambda: nc.vector.tensor_reduce(
                        out=ST3[:, 0:8], in_=SSB[:, 0:32].rearrange("p (a b) -> p a b", b=4), axis=AX.X, op=ALU.add),
                        reads=[rSSB], writes=[rst3[0]], dur=0.2)
                    K.op(ACT, lambda: nc.scalar.activation(out=ST3[:, 8:16], in_=ST3[:, 0:8], func=AF.Sqrt,
                                                           bias=EPSC, scale=1.0 / 512),
                         reads=[rst3[0]], writes=[rst3[1]], tbl="sqrt", dur=0.31)
                    K.op(DVE, lambda: nc.vector.reciprocal(out=ST3[:, 16:24], in_=ST3[:, 8:16]),
                         reads=[rst3[1]], writes=[rst3[2]], dur=0.2)
                    def s6(t=t):
                        for b in range(2):
                            K.dma(SP, chXR[b], XR[b], rows(src, tok_s + t * TT + b * 128), writes=[rXR[b]])
                        for b in range(4):
                            slot2 = b % 2
                            for hf in range(2):
                                def fol(b=b, hf=hf):
                                    ins = None
                                    for c in range(4):
                                        ins = nc.tensor.matmul(OL, lhsT=VB[:, c, b * 128:(b + 1) * 128],
                                                               rhs=WOUT[:, c, hf * 512:(hf + 1) * 512],
                                                               start=(c == 0), stop=(c == 3))
                                    return ins
                                K.op(PE, fol, reads=rVB + [rWOUT], writes=[rOL], dur=0.88)

                                def fop(b=b, hf=hf):
                                    ins = None
                                    for g in range(4):
                                        ins = nc.tensor.matmul(OP, lhsT=PB[:, g, b * 128:(b + 1) * 128],
                                                               rhs=WOUT[:, 4 + g, hf * 512:(hf + 1) * 512],
                                                               start=(g == 0), stop=(g == 3))
                                    return ins
                                K.op(PE, fop, reads=rPB + [rWOUT], writes=[rOP], dur=0.88)
                                oh = O[slot2][:, hf * 512:(hf + 1) * 512]
                                K.op(ACT, lambda oh=oh, b=b: nc.scalar.activation(out=oh, in_=OL, func=AF.Copy,
                                                                                  scale=ST3[:, 16 + b:17 + b]),
                                     reads=[rOL, rst3[2]], writes=[rO[slot2][hf]])
                                K.op(DVE, lambda oh=oh, b=b: nc.vector.scalar_tensor_tensor(
                                    out=oh, in0=OP, scalar=ST3[:, 20 + b:21 + b], in1=oh, op0=ALU.mult, op1=ALU.add),
                                    reads=[rOP, rst3[2], rO[slot2][hf]], writes=[rO[slot2][hf]])
                            K.op(ACT, lambda slot2=slot2, b=b: nc.scalar.activation(
                                out=ZX[:, 0, 0:512].bitcast(BF16), in_=O[slot2], func=AF.Square,
                                accum_out=ST3[:, 24 + b:25 + b]),
                                reads=rO[slot2], writes=[rst3[3 + b], rZX[0]], n=1024)
                            K.op(ACT, lambda b=b: nc.scalar.activation(out=ST3[:, 28 + b:29 + b], in_=ST3[:, 24 + b:25 + b],
                                                                       func=AF.Sqrt, bias=EPSC, scale=1.0 / D),
                                 reads=[rst3[3 + b]], writes=[rst3[7 + b]], tbl="sqrt", dur=0.31)
                            K.op(DVE, lambda b=b: nc.vector.reciprocal(out=ST3[:, 32 + b:33 + b], in_=ST3[:, 28 + b:29 + b]),
                                 reads=[rst3[7 + b]], writes=[rst3[11 + b]], dur=0.2)
                            K.op(DVE, lambda slot2=slot2, b=b: nc.vector.scalar_tensor_tensor(
                                out=O[slot2], in0=O[slot2], scalar=ST3[:, 32 + b:33 + b], in1=GBm,
                                op0=ALU.mult, op1=ALU.mult),
                                reads=rO[slot2] + [rst3[11 + b], rGBm], writes=rO[slot2], n=1024)
                            K.op(POOL, lambda slot2=slot2: nc.gpsimd.tensor_tensor(out=XR[slot2], in0=O[slot2],
                                                                                   in1=XR[slot2], op=ALU.add),
                                 reads=rO[slot2] + [rXR[slot2]], writes=[rXR[slot2]], n=1024,
                                 alts=[(DVE, lambda slot2=slot2: nc.vector.tensor_tensor(
                                     out=XR[slot2], in0=O[slot2], in1=XR[slot2], op=ALU.add))])
                            K.dma(SP, chST[slot2], rows(dst, tok_s + t * TT + b * 128), XR[slot2], reads=[rXR[slot2]])
                            if b + 2 < 4:
                                K.dma(SP, chXR[slot2], XR[slot2], rows(src, tok_s + t * TT + (b + 2) * 128),
                                      writes=[rXR[slot2]])
                    pend.append(s6)
                    if t + 1 < NT:
                        slot = nslot
                        bsl = nbsl
                while pend:
                    pend.pop(0)()
            K.flush(window=SCHED_WINDOW, verbose=True)
            K.barrier()

        ffn_phase(x_d, s1_d, w1a_d, w1b_d, 0, 0, "fa", True)
        mixer_phase(s1_d, s2_d)
        ffn_phase(s2_d, y_d, w2a_d, w2b_d, 16, 2, "fc", False)
        K.barrier()
    return nc


def host_prep(inputs, nseq_total):
    f = lambda a: np.ascontiguousarray(np.asarray(a, dtype=np.float32))
    pv = np.zeros((128, NPV), np.float32)

    def put(col0, vec, n):
        pv[:, col0:col0 + n] = f(vec).reshape(n, 128).T

    put(0, inputs["ffn1_pre_g"][0], 8)
    put(8, inputs["mix_pre_g"][0], 8)
    put(16, inputs["ffn2_pre_g"][0], 8)
    cw = f(inputs["conv_w"][0])
    for k in range(4):
        put(24 + 4 * k, cw[k], 4)
    put(40, inputs["conv_b"][0], 4)
    for dr in range(2):
        put(44 + 4 * dr, inputs["lru_b_a"][0][dr], 4)
        put(52 + 4 * dr, inputs["lru_b_x"][0][dr], 4)
        put(60 + 4 * dr, inputs["lru_lam"][0][dr], 4)
    put(68, inputs["pool_scale"][0], 4)
    put(72, inputs["lru_out_g"][0], 4)
    put(76, inputs["pool_out_g"][0], 4)
    gw = np.zeros((128, 16, 128), np.float32)
    wa = f(inputs["lru_w_a"][0])
    wx = f(inputs["lru_w_x"][0])
    for dr in range(2):
        for gi, wsrc in enumerate((wa, wx)):
            for c in range(4):
                for hh in range(2):
                    gw[hh * 64:(hh + 1) * 64, dr * 8 + gi * 4 + c, hh * 64:(hh + 1) * 64] = wsrc[dr, 2 * c + hh]
    pw = np.ascontiguousarray(f(inputs["pool_w"][0]).transpose(1, 0, 2))
    gbc = np.zeros((128, 3, D), np.float32)
    gbc[:, 0, :] = f(inputs["ffn1_post_g"][0])[None, :]
    gbc[:, 1, :] = f(inputs["mix_post_g"][0])[None, :]
    gbc[:, 2, :] = f(inputs["ffn2_post_g"][0])[None, :]
    shared = {
        "w1a": f(inputs["ffn1_w_in"][0]), "w1b": f(inputs["ffn1_w_out"][0]),
        "w2a": f(inputs["ffn2_w_in"][0]), "w2b": f(inputs["ffn2_w_out"][0]),
        "win": f(inputs["w_in"][0]), "wout": f(inputs["w_out"][0]),
        "gw": gw.reshape(128, 16 * 128), "pw": pw.reshape(128, 4 * 128),
        "pvec": pv, "gbc": gbc.reshape(128, 3 * D), "ident": np.eye(128, dtype=np.float32),
    }
    return shared


def run(inputs, n_cores=8):
    xp = np.asarray(inputs["x_prompt"], dtype=np.float32)
    xs = np.asarray(inputs["x_sample"], dtype=np.float32)
    S = xp.shape[1]
    assert xs.shape[1] == S
    allx = np.concatenate([xp, xs], axis=0)
    ntot = allx.shape[0]
    assert ntot % n_cores == 0
    nseq = ntot // n_cores
    shared = host_prep(inputs, ntot)
    nc = build(nseq, S)
    in_maps = []
    for c in range(n_cores):
        m = dict(shared)
        m["x"] = np.ascontiguousarray(allx[c * nseq:(c + 1) * nseq].reshape(nseq * S, D))
        in_maps.append(m)
    res = run_bass_kernel_spmd(nc, in_maps, core_ids=list(range(n_cores)))
    outs = [np.asarray(r["y"], dtype=np.float32).reshape(nseq, S, D) for r in res.results]
    ally = np.concatenate(outs, axis=0)
    return ally[:xp.shape[0]], ally[xp.shape[0]:]


def kernel(**inputs):
    yp, ys = run(inputs, 8)
    return (np.ascontiguousarray(yp), np.ascontiguousarray(ys))
```

```python
import numpy as np
from contextlib import ExitStack
import concourse.bass as bass
import concourse.mybir as mybir
from concourse.bass_utils import run_bass_kernel_spmd

F32 = mybir.dt.float32
BF16 = mybir.dt.bfloat16
AF = mybir.ActivationFunctionType
ALU = mybir.AluOpType
AX = mybir.AxisListType

D = 1024
DFF = 2816
NF = DFF // 128
NKD = D // 128
TT = 512
EPS = 1e-6
NPV = 96
WINS = (2, 4, 8, 16)
ARENA_WORDS = 53200
SCHED_WINDOW = 700
SCHED_FFN = True
TBL_BIAS = 1.0
BAN_POOL = False
ALT_ENGINES = ("DVE", "POOL")


class Chan:
    __slots__ = ("sem", "count")

    def __init__(self, sem):
        self.sem = sem
        self.count = 0


class Eng:
    def __init__(self, e, ch):
        self.e = e
        self.ch = ch
        self.seen = {}


class Res:
    __slots__ = ("lw", "rd")

    def __init__(self):
        self.lw = None
        self.rd = {}


def RL(n):
    return [Res() for _ in range(n)]


class KB:
    def __init__(self, nc, es):
        self.nc = nc
        self.es = es
        self.chans = []
        self.rec = None
        self.PE = Eng(nc.tensor, self.chan("c_pe"))
        self.ACT = Eng(nc.scalar, self.chan("c_act"))
        self.DVE = Eng(nc.vector, self.chan("c_dve"))
        self.POOL = Eng(nc.gpsimd, self.chan("c_pool"))
        self.SP = Eng(nc.sync, self.chan("c_sp"))
        self.engs = [self.PE, self.ACT, self.DVE, self.POOL, self.SP]
        self.alt_engines = [e for e, nm in ((self.DVE, "DVE"), (self.POOL, "POOL"), (self.ACT, "ACT")) if nm in ALT_ENGINES]

    def chan(self, name):
        c = Chan(self.es.enter_context(self.nc.semaphore(name)))
        self.chans.append(c)
        return c

    def _sync(self, eng, reads, writes):
        deps = {}
        for r in reads:
            if r.lw is not None:
                ch, c = r.lw
                if deps.get(ch, 0) < c:
                    deps[ch] = c
        for w in writes:
            if w.lw is not None:
                ch, c = w.lw
                if deps.get(ch, 0) < c:
                    deps[ch] = c
            for ch, c in w.rd.items():
                if deps.get(ch, 0) < c:
                    deps[ch] = c
        for ch, c in deps.items():
            if eng.seen.get(ch, 0) < c:
                eng.e.wait_ge(ch.sem, c)
                eng.seen[ch] = c

    def op(self, eng, fn, reads=(), writes=(), dur=None, n=512, cls=None, tbl=None, alts=None):
        if self.rec is not None:
            if dur is None:
                dur = self.est(eng, n, cls)
            al = [(eng, fn, dur)]
            for (e2, f2) in (alts or ()):
                if e2 in self.alt_engines:
                    al.append((e2, f2, self.est(e2, n, cls)))
            if BAN_POOL and eng is self.POOL and len(al) > 1:
                al = al[1:]
                eng, fn, dur = al[0]
            self.rec.append(dict(kind="op", eng=eng, fn=fn, reads=list(reads), writes=list(writes), dur=dur,
                                 occ=dur, tbl=tbl, alts=al))
            return
        self._sync(eng, reads, writes)
        ins = fn()
        ch = eng.ch
        ch.count += 1
        ins.then_inc(ch.sem, 1)
        c = ch.count
        for r in reads:
            r.rd[ch] = c
        for w in writes:
            w.lw = (ch, c)
            w.rd = {}

    def dma(self, qeng, chan, out, in_, reads=(), writes=(), nbytes=524288, **kw):
        if self.rec is not None:
            self.rec.append(dict(kind="dma", eng=qeng, chan=chan, out=out, in_=in_, reads=list(reads),
                                 writes=list(writes), kw=kw, dur=2.5 + nbytes / 150e3, occ=0.15, tbl=None))
            return
        self._sync(qeng, reads, writes)
        ins = qeng.e.dma_start(out=out, in_=in_, **kw)
        chan.count += 16
        ins.then_inc(chan.sem, 16)
        c = chan.count
        for r in reads:
            r.rd[chan] = c
        for w in writes:
            w.lw = (chan, c)
            w.rd = {}

    def est(self, eng, n, cls):
        if eng is self.ACT:
            return 0.30 + n / 1200.0
        if eng is self.DVE:
            if cls == "scan":
                return 0.2 + 2.8 * n / 960.0
            if cls == "fast":
                return 0.16 + n / 1920.0
            return 0.2 + 1.5 * n / 960.0
        if eng is self.POOL:
            return 0.3 + 2.6 * n / 1000.0
        if eng is self.PE:
            return 0.22
        return 0.2

    def begin_record(self):
        self.rec = []

    def flush(self, window=700, verbose=False):
        rec = self.rec
        self.rec = None
        n = len(rec)
        if n == 0:
            return
        lastw = {}
        readers = {}
        preds = [set() for _ in range(n)]
        for i, o in enumerate(rec):
            for r in o["reads"]:
                k = id(r)
                if k in lastw:
                    preds[i].add(lastw[k])
            for w in o["writes"]:
                k = id(w)
                if k in lastw:
                    preds[i].add(lastw[k])
                for j in readers.get(k, ()):
                    preds[i].add(j)
            for r in o["reads"]:
                readers.setdefault(id(r), []).append(i)
            for w in o["writes"]:
                k = id(w)
                lastw[k] = i
                readers[k] = []
            preds[i].discard(i)
        succs = [[] for _ in range(n)]
        for i in range(n):
            for j in preds[i]:
                succs[j].append(i)
        prio = [0.0] * n
        for i in range(n - 1, -1, -1):
            m = 0.0
            for j in succs[i]:
                if prio[j] > m:
                    m = prio[j]
            prio[i] = m + rec[i]["dur"]
        npred = [len(p) for p in preds]
        finish = [0.0] * n
        free = {}
        cand = set(i for i in range(n) if npred[i] == 0)
        done = [False] * n
        lo = 0
        order = []
        act_tbl = [None]
        LAT = 0.15
        while len(order) < n:
            while lo < n and done[lo]:
                lo += 1
            best = None
            bkey = None
            for i in cand:
                if i >= lo + window:
                    continue
                o = rec[i]
                rdy = 0.0
                for j in preds[i]:
                    f = finish[j] + LAT
                    if f > rdy:
                        rdy = f
                if o["kind"] == "dma":
                    e = o["eng"]
                    st = max(free.get(e, 0.0), rdy)
                    key = (st, -prio[i], i)
                    if bkey is None or key < bkey:
                        bkey = key
                        best = (i, st, 0.0, None)
                    continue
                bfin = None
                for ai, (e, f_, du) in enumerate(o["alts"]):
                    st = max(free.get(e, 0.0), rdy)
                    pen = 0.0
                    if o["tbl"] is not None and e is self.ACT and act_tbl[0] != o["tbl"]:
                        pen = 1.3
                    fin = st + pen + du
                    if bfin is None or fin < bfin[0]:
                        bfin = (fin, st, pen, ai)
                key = (bfin[1] + bfin[2] * TBL_BIAS, -prio[i], i)
                if bkey is None or key < bkey:
                    bkey = key
                    best = (i, bfin[1], bfin[2], bfin[3])
            i, st, pen, ai = best
            o = rec[i]
            if ai is not None:
                e, f_, du = o["alts"][ai]
                o["eng"], o["fn"], o["dur"], o["occ"] = e, f_, du, du
            e = o["eng"]
            if o["tbl"] is not None and e is self.ACT:
                if act_tbl[0] != o["tbl"]:
                    self.nswitch = getattr(self, "nswitch", 0) + 1
                act_tbl[0] = o["tbl"]
            st += pen
            free[e] = st + o["occ"]
            finish[i] = st + o["dur"]
            done[i] = True
            cand.discard(i)
            order.append(i)
            for j in succs[i]:
                npred[j] -= 1
                if npred[j] == 0:
                    cand.add(j)
        self.last_makespan = max(finish)
        if verbose:
            busy = {}
            for o in rec:
                busy[o["eng"]] = busy.get(o["eng"], 0.0) + o["occ"]
            print("[sched] ops=%d makespan=%.1fus critpath=%.1fus tblsw=%d busy=%s" % (
                n, self.last_makespan, max(prio), getattr(self, "nswitch", 0), {("PE", "ACT", "DVE", "POOL", "SP")[self.engs.index(e)]: round(b, 1)
                                        for e, b in busy.items()}))
        for i in order:
            o = rec[i]
            if o["kind"] == "op":
                self.op(o["eng"], o["fn"], o["reads"], o["writes"])
            else:
                self.dma(o["eng"], o["chan"], o["out"], o["in_"], o["reads"], o["writes"], **o["kw"])

    def barrier(self):
        for eng in self.engs:
            for ch in self.chans:
                if ch.count > eng.seen.get(ch, 0):
                    eng.e.wait_ge(ch.sem, ch.count)
                    eng.seen[ch] = ch.count


class Arena:
    def __init__(self, ap, nwords):
        self.ap = ap
        self.n = nwords
        self.off = 0

    def view(self, off, shape, dtype):
        n = int(np.prod(shape))
        words = n if dtype == F32 else (n + 1) // 2
        assert off + words <= self.n, (off, words, self.n)
        v = self.ap[:, off:off + words]
        if dtype != F32:
            v = v.bitcast(dtype)
        if len(shape) == 2:
            v = v.rearrange("p (a b) -> p a b", a=shape[0])
        return v, words

    def alloc(self, shape, dtype=F32):
        v, words = self.view(self.off, shape, dtype)
        self.off += words
        return v


def build(nseq, S):
    NT = S // TT
    ntok = nseq * S
    nc = bass.Bass("TRN2", target_bir_lowering=False)

    def din(name, shape):
        return nc.dram_tensor(name, shape, F32, kind="ExternalInput").ap()

    x_d = din("x", [ntok, D])
    y_d = nc.dram_tensor("y", [ntok, D], F32, kind="ExternalOutput").ap()
    s1_d = nc.dram_tensor("s1", [ntok, D], F32, kind="Internal").ap()
    s2_d = nc.dram_tensor("s2", [ntok, D], F32, kind="Internal").ap()
    ht_d = nc.dram_tensor("ht", [NKD, 128, ntok], BF16, kind="Internal").ap()
    hb_d = nc.dram_tensor("hb", [4, 128, ntok], F32, kind="Internal").ap()
    xc_d = nc.dram_tensor("xc", [4, 128, ntok], F32, kind="Internal").ap()
    w1a_d = din("w1a", [D, 2 * DFF])
    w1b_d = din("w1b", [DFF, D])
    w2a_d = din("w2a", [D, 2 * DFF])
    w2b_d = din("w2b", [DFF, D])
    win_d = din("win", [D, 1536])
    wout_d = din("wout", [D, D])
    gw_d = din("gw", [128, 16 * 128])
    pw_d = din("pw", [128, 4 * 128])
    pvec_d = din("pvec", [128, NPV])
    gbc_d = din("gbc", [128, 3 * D])
    ident_d = din("ident", [128, 128])

    with ExitStack() as es:
        arena_t = es.enter_context(nc.sbuf_tensor("arena", [128, ARENA_WORDS], F32))
        banks = [es.enter_context(nc.psum_tensor("bank%d" % i, [128, 512], F32)) for i in range(8)]
        K = KB(nc, es)
        PE, ACT, DVE, POOL, SP = K.PE, K.ACT, K.DVE, K.POOL, K.SP
        ar = Arena(arena_t[:, :], ARENA_WORDS)
        P = [b[:, :] for b in banks]
        rP = RL(8)

        PV = ar.alloc((NPV,), F32)
        IDB = ar.alloc((128,), BF16)
        ONEB = ar.alloc((2,), BF16)
        persist_off = ar.off
        rPV, rID, rONE = Res(), Res(), Res()
        ch_c = K.chan("ch_const")
        K.dma(SP, ch_c, PV, pvec_d, writes=[rPV])
        ch_c2 = K.chan("ch_const2")
        K.dma(POOL, ch_c2, IDB, ident_d, writes=[rID])
        K.op(DVE, lambda: nc.vector.memset(ONEB, 1.0), writes=[rONE])
        K.op(DVE, lambda: nc.vector.memset(PV[:, 92:93], EPS), reads=[], writes=[rPV])
        K.op(DVE, lambda: nc.vector.memset(PV[:, 93:94], 1.0), reads=[], writes=[rPV])
        K.op(ACT, lambda: nc.scalar.activation(out=PV[:, 80:88], in_=PV[:, 60:68], func=AF.Exp, scale=-1.0),
             reads=[rPV], writes=[rPV])
        K.op(ACT, lambda: nc.scalar.activation(out=PV[:, 80:88], in_=PV[:, 80:88], func=AF.Ln,
                                               bias=PV[:, 93:94], scale=1.0), reads=[rPV], writes=[rPV])
        K.op(DVE, lambda: nc.vector.tensor_scalar(out=PV[:, 80:88], in0=PV[:, 80:88], scalar1=-8.0,
                                                  scalar2=None, op0=ALU.mult), reads=[rPV], writes=[rPV])
        K.op(DVE, lambda: nc.vector.tensor_tensor(out=PV[:, 88:92], in0=PV[:, 68:72], in1=PV[:, 76:80],
                                                  op=ALU.mult), reads=[rPV], writes=[rPV])
        K.barrier()
        EPSC = PV[:, 92:93]
        ONEC = PV[:, 93:94]

        def rows(dram, tok0, n=128):
            return dram[tok0:tok0 + n, :]

        def norm_block(Xb, rXb, Hb, rHb, ST, rst, b):
            K.op(ACT, lambda: nc.scalar.activation(out=Hb, in_=Xb, func=AF.Square, accum_out=ST[:, b:b + 1]),
                 reads=[rXb], writes=[rst[b], rHb], n=1024)
            K.op(ACT, lambda: nc.scalar.activation(out=ST[:, 4 + b:5 + b], in_=ST[:, b:b + 1], func=AF.Sqrt,
                                                   bias=EPSC, scale=1.0 / D),
                 reads=[rst[b]], writes=[rst[4 + b]], tbl="sqrt", dur=0.31)
            K.op(DVE, lambda: nc.vector.reciprocal(out=ST[:, 8 + b:9 + b], in_=ST[:, 4 + b:5 + b]),
                 reads=[rst[4 + b]], writes=[rst[8 + b]], dur=0.2)
            K.op(DVE, lambda: nc.vector.tensor_scalar(out=Hb, in0=Xb, scalar1=ST[:, 8 + b:9 + b],
                                                      scalar2=None, op0=ALU.mult),
                 reads=[rXb, rst[8 + b]], writes=[rHb], n=1024, cls="fast")

        def ffn_phase(src, dst, wa_d, wb_d, pgcol, gbi, tag, emit_ht):
            ar.off = persist_off
            W1 = ar.alloc((NKD, 2 * DFF), BF16)
            W2 = ar.alloc((NF, D), BF16)
            GB = ar.alloc((D,), F32)
            XS = [ar.alloc((D,), F32) for _ in range(2)]
            H = ar.alloc((4, D), BF16)
            HT = ar.alloc((NKD, TT), BF16)
            actb_off = ar.off
            ACTB = ar.alloc((NF, TT), BF16)
            SG = [ar.alloc((TT,), F32) for _ in range(2)]
            yb_off = ar.off
            YB = [ar.alloc((D,), F32) for _ in range(2)]
            xr_off = ar.off
            XR = [ar.alloc((D,), F32) for _ in range(2)]
            ST = ar.alloc((16,), F32)
            ST2 = ar.alloc((16,), F32)
            ST4 = ar.alloc((16,), F32)
            H2 = ar.alloc((D,), BF16)
            HTB = ar.alloc((NKD, 128), BF16)
            STG = [ar.view(actb_off + h * DFF, (DFF,), F32)[0] for h in range(2)]
            rW1 = RL(16)
            rW2, rGB, rH2, rHTB = Res(), Res(), Res(), Res()
            rXS, rH, rHT, rACT = RL(2), RL(4), RL(NKD), RL(NF)
            rSG, rXR, rSTG = RL(2), RL(2), RL(2)
            rYB = [RL(2), RL(2)]
            rst, rst2, rst4 = RL(16), RL(16), RL(16)
            chXS = [K.chan("%s_xs%d" % (tag, i)) for i in range(2)]
            chXR = [K.chan("%s_xr%d" % (tag, i)) for i in range(2)]
            chST = [K.chan("%s_st%d" % (tag, i)) for i in range(2)]
            chW = K.chan("%s_w" % tag)
            chGB = K.chan("%s_gb" % tag)
            chHT = K.chan("%s_ht" % tag)
            chSTG = [K.chan("%s_stg%d" % (tag, i)) for i in range(2)]
            PT = [P[6].bitcast(BF16)[:, 0:512], P[7].bitcast(BF16)[:, 0:512]]
            PTX = [P[6].bitcast(BF16), P[7].bitcast(BF16)]
            rPT = [rP[6], rP[7]]
            G, U, Y = [P[0], P[1]], [P[2], P[3]], [P[4], P[5]]
            rG, rU, rY = [rP[0], rP[1]], [rP[2], rP[3]], [rP[4], rP[5]]

            if SCHED_FFN:
                K.begin_record()
            NJB = NF // 2
            rW1p = [RL(NJB), RL(NJB)]
            STGV = [ar.view(yb_off, (NKD, 256), F32)[0], ar.view(xr_off, (NKD, 256), F32)[0]]
            rSTGV = [rYB[0] + rYB[1], list(rXR)]
            K.dma(SP, chGB, GB, gbc_d[:, gbi * D:(gbi + 1) * D], writes=[rGB], nbytes=524288)
            K.op(DVE, lambda: nc.vector.tensor_scalar(out=GB, in0=GB, scalar1=0.5, scalar2=None, op0=ALU.mult),
                 reads=[rGB], writes=[rGB], n=1024, cls="fast")
            pi = 0
            for jb in range(NJB):
                for h in range(2):
                    sl = pi % 2
                    col = h * DFF + jb * 256
                    K.dma(SP, chSTG[sl], STGV[sl], wa_d[:, col:col + 256].rearrange("(k p) c -> p k c", p=128),
                          writes=list(rSTGV[sl]), nbytes=1048576)
                    for k in range(NKD):
                        if k % 2 == 0:
                            K.op(DVE, lambda k=k, sl=sl, col=col: nc.vector.tensor_scalar(
                                out=W1[:, k, col:col + 256], in0=STGV[sl][:, k, :],
                                scalar1=PV[:, pgcol + k:pgcol + k + 1], scalar2=None, op0=ALU.mult),
                                reads=list(rSTGV[sl]) + [rPV], writes=[rW1p[h][jb]], n=256, cls="fast")
                        else:
                            K.op(ACT, lambda k=k, sl=sl, col=col: nc.scalar.activation(
                                out=W1[:, k, col:col + 256], in_=STGV[sl][:, k, :], func=AF.Copy,
                                scale=PV[:, pgcol + k:pgcol + k + 1]),
                                reads=list(rSTGV[sl]) + [rPV], writes=[rW1p[h][jb]], n=256)
                    pi += 1
            for f0 in range(0, NF, 2):
                K.dma(POOL, chW, W2[:, f0:f0 + 2, :],
                      wb_d[f0 * 128:(f0 + 2) * 128, :].rearrange("(f p) d -> p f d", p=128),
                      reads=[rW1p[1][NJB // 2]], writes=[rW2], nbytes=1048576)

            ntile = ntok // TT

            def do_pre_block(t, b):
                slot = b % 2
                K.dma(SP, chXS[slot], XS[slot], rows(src, t * TT + b * 128), writes=[rXS[slot]])
                norm_block(XS[slot], rXS[slot], H[:, b, :], rH[b], ST, rst, b)

            def do_T(t):
                for k in range(NKD):
                    pt = PT[k % 2]

                    def f(k=k, pt=pt):
                        ins = None
                        for b in range(4):
                            ins = nc.tensor.transpose(out=pt[:, b * 128:(b + 1) * 128],
                                                      in_=H[:, b, k * 128:(k + 1) * 128], identity=IDB)
                        return ins
                    K.op(PE, f, reads=list(rH) + [rID], writes=[rPT[k % 2]], dur=0.5)
                    if k % 2 == 0:
                        K.op(ACT, lambda k=k, pt=pt: nc.scalar.copy(out=HT[:, k, :], in_=pt),
                             reads=[rPT[k % 2]], writes=[rHT[k]])
                    else:
                        K.op(DVE, lambda k=k, pt=pt: nc.vector.tensor_copy(out=HT[:, k, :], in_=pt),
                             reads=[rPT[k % 2]], writes=[rHT[k]])

            pending = []

            def emit_ht_T():
                tok0, par = pending.pop(0)
                ptx = PTX[par]

                def f():
                    ins = None
                    for k in range(NKD):
                        ins = nc.tensor.transpose(out=ptx[:, k * 128:(k + 1) * 128],
                                                  in_=H2[:, k * 128:(k + 1) * 128], identity=IDB)
                    return ins
                K.op(PE, f, reads=[rH2, rID], writes=[rPT[par]], dur=0.9)
                K.op(DVE, lambda: nc.vector.tensor_copy(out=HTB, in_=ptx.rearrange("p (k n) -> p k n", k=NKD)),
                     reads=[rPT[par]], writes=[rHTB], n=1024, cls="fast")
                K.dma(SP, chHT, ht_d[:, :, tok0:tok0 + 128].rearrange("k p n -> p k n"), HTB, reads=[rHTB])

            for b in range(4):
                do_pre_block(0, b)
            do_T(0)
            blkctr = 0

            for t in range(ntile):
                pre_at = {3: 0, 8: 1, 13: 2, 18: 3} if t + 1 < ntile else {}
                for j in range(NF):
                    def fg(j=j, col=j * 128, bank=G[j % 2]):
                        ins = None
                        for k in range(NKD):
                            ins = nc.tensor.matmul(bank, lhsT=W1[:, k, col:col + 128], rhs=HT[:, k, :],
                                                   start=(k == 0), stop=(k == NKD - 1))
                        return ins
                    K.op(PE, fg, reads=rHT + [rW1p[0][j // 2]], writes=[rG[j % 2]], dur=1.75)

                    def fu(j=j, col=DFF + j * 128, bank=U[j % 2]):
                        ins = None
                        for k in range(NKD):
                            ins = nc.tensor.matmul(bank, lhsT=W1[:, k, col:col + 128], rhs=HT[:, k, :],
                                                   start=(k == 0), stop=(k == NKD - 1))
                        return ins
                    K.op(PE, fu, reads=rHT + [rW1p[1][j // 2]], writes=[rU[j % 2]], dur=1.75)
                    K.op(ACT, lambda j=j: nc.scalar.activation(out=SG[j % 2], in_=G[j % 2], func=AF.Silu),
                         reads=[rG[j % 2]], writes=[rSG[j % 2]], tbl="silu")
                    K.op(DVE, lambda j=j: nc.vector.tensor_tensor(out=ACTB[:, j, :], in0=U[j % 2], in1=SG[j % 2],
                                                                  op=ALU.mult),
                         reads=[rU[j % 2], rSG[j % 2]], writes=[rACT[j]])
                    if j == 1 and pending:
                        emit_ht_T()
                    if j in pre_at:
                        do_pre_block(t + 1, pre_at[j])
                if t + 1 < ntile:
                    do_T(t + 1)
                for b in range(2):
                    K.dma(SP, chXR[b], XR[b], rows(src, t * TT + b * 128), writes=[rXR[b]])
                for b in range(4):
                    slot = b % 2
                    for hf in range(2):
                        def fy(b=b, hf=hf):
                            ins = None
                            for f in range(NF):
                                ins = nc.tensor.matmul(Y[hf], lhsT=ACTB[:, f, b * 128:(b + 1) * 128],
                                                       rhs=W2[:, f, hf * 512:(hf + 1) * 512],
                                                       start=(f == 0), stop=(f == NF - 1))
                            return ins
                        K.op(PE, fy, reads=rACT + [rW2], writes=[rY[hf]], dur=4.8)
                        if hf == 0:
                            K.op(ACT, lambda slot=slot, hf=hf: nc.scalar.copy(
                                out=YB[slot][:, hf * 512:(hf + 1) * 512], in_=Y[hf]),
                                reads=[rY[hf]], writes=[rYB[slot][hf]])
                        else:
                            K.op(DVE, lambda slot=slot, hf=hf: nc.vector.tensor_copy(
                                out=YB[slot][:, hf * 512:(hf + 1) * 512], in_=Y[hf]),
                                reads=[rY[hf]], writes=[rYB[slot][hf]])
                    if pending:
                        emit_ht_T()
                    K.op(ACT, lambda slot=slot, b=b: nc.scalar.activation(
                        out=SG[0].bitcast(BF16), in_=YB[slot], func=AF.Square, accum_out=ST2[:, b:b + 1]),
                        reads=rYB[slot], writes=[rst2[b], rSG[0]], n=1024)
                    K.op(ACT, lambda b=b: nc.scalar.activation(out=ST2[:, 4 + b:5 + b], in_=ST2[:, b:b + 1],
                                                               func=AF.Sqrt, bias=EPSC, scale=1.0 / D),
                         reads=[rst2[b]], writes=[rst2[4 + b]], tbl="sqrt", dur=0.31)
                    K.op(DVE, lambda b=b: nc.vector.reciprocal(out=ST2[:, 8 + b:9 + b], in_=ST2[:, 4 + b:5 + b]),
                         reads=[rst2[4 + b]], writes=[rst2[8 + b]], dur=0.2)
                    K.op(DVE, lambda slot=slot, b=b: nc.vector.scalar_tensor_tensor(
                        out=YB[slot], in0=YB[slot], scalar=ST2[:, 8 + b:9 + b], in1=GB,
                        op0=ALU.mult, op1=ALU.mult),
                        reads=rYB[slot] + [rst2[8 + b], rGB], writes=rYB[slot], n=1024)
                    K.op(POOL, lambda slot=slot: nc.gpsimd.tensor_tensor(out=XR[slot], in0=YB[slot], in1=XR[slot],
                                                                         op=ALU.add),
                         reads=rYB[slot] + [rXR[slot]], writes=[rXR[slot]], n=1024)
                    tok0 = t * TT + b * 128
                    K.dma(SP, chST[slot], rows(dst, tok0), XR[slot], reads=[rXR[slot]])
                    if emit_ht:
                        norm_block(XR[slot], rXR[slot], H2, rH2, ST4, rst4, b)
                        pending.append((tok0, blkctr % 2))
                        blkctr += 1
                    if b + 2 < 4:
                        K.dma(SP, chXR[slot], XR[slot], rows(src, t * TT + (b + 2) * 128), writes=[rXR[slot]])
            while pending:
                emit_ht_T()
            if SCHED_FFN:
                K.flush(window=SCHED_WINDOW, verbose=True)
            K.barrier()

        def mixer_phase(src, dst):
            ar.off = persist_off
            WIN = ar.alloc((NKD, 1536), BF16)
            WOUT = ar.alloc((NKD, D), BF16)
            GW = ar.alloc((16, 128), BF16)
            PWT = ar.alloc((4, 128), BF16)
            GBm = ar.alloc((D,), F32)
            CB = ar.alloc((NT, 4), F32)
            CF = ar.alloc((4,), F32)
            CBR = ar.alloc((4,), F32)
            sub_off = ar.off
            HW = [ar.alloc((NKD, 528), BF16) for _ in range(2)]
            ZX = ar.alloc((4, 528), F32)
            ZP = ar.alloc((4, 528), F32)
            GEL = ar.alloc((4, TT), F32)
            XC = ar.alloc((4, TT), F32)
            XCB = ar.alloc((4, TT), BF16)
            ta_off = ar.off
            TA = [[ar.alloc((TT,), F32) for _ in range(3)] for _ in range(8)]
            HF = ar.alloc((4, TT), F32)
            VB = ar.alloc((4, TT), BF16)
            PB = ar.alloc((4, TT), BF16)
            SQ = [ar.alloc((TT,), BF16) for _ in range(4)]
            PA = [ar.alloc((528,), F32) for _ in range(2)]
            DD = ar.alloc((4, TT), BF16)
            O = [ar.alloc((D,), F32) for _ in range(2)]
            XR = [ar.alloc((D,), F32) for _ in range(2)]
            ST3 = ar.alloc((40,), F32)
            rWIN, rWOUT, rGW, rPWT, rGBm = Res(), Res(), Res(), Res(), Res()
            rCB, rCF, rCBR = RL(NT), RL(4), RL(4)
            rHW = RL(2)
            rZX, rZP, rGEL, rXC, rXCB = RL(4), RL(4), RL(4), RL(4), RL(4)
            rTA = [RL(3) for _ in range(8)]
            rHF, rVB, rPB, rDD = RL(4), RL(4), RL(4), RL(4)
            rSQ, rPA, rXR = RL(4), RL(2), RL(2)
            rO = [RL(2), RL(2)]
            rst3 = RL(40)

            def ta_view(j0):
                v, _ = ar.view(ta_off + j0 * TT, (4, TT), F32)
                return v, [rTA[j // 3][j % 3] for j in range(j0, j0 + 4)]
            HBS0, rHBS0 = ta_view(12)
            HBS1, rHBS1 = ta_view(16)
            XC2, rXC2 = ta_view(20)
            HBS, rHBS = [HBS0, HBS1], [rHBS0, rHBS1]
            XCS, rXCS = [XC, XC2], [rXC, rXC2]
            chXCS = [K.chan("mx_xcs%d" % i) for i in range(2)]
            chHBS = [K.chan("mx_hbs%d" % i) for i in range(8)]
            chXCL = [K.chan("mx_xcl%d" % i) for i in range(2)]
            chHBL = [K.chan("mx_hbl%d" % i) for i in range(2)]
            chW = K.chan("mx_w")
            chSTG = K.chan("mx_stg")
            chGBm = K.chan("mx_gb")
            chHW = [K.chan("mx_hw%d" % i) for i in range(2)]
            chXR = [K.chan("mx_xr%d" % i) for i in range(2)]
            chST = [K.chan("mx_st%d" % i) for i in range(2)]
            STGs = [ar.view(sub_off, (1536,), F32)[0], ar.view(sub_off + 1536, (1536,), F32)[0]]
            rSTGs = RL(2)
            chSTGs = [chSTG, K.chan("mx_stg2")]
            steps = [(WIN[:, k, :], win_d[k * 128:(k + 1) * 128, :], 1536, 8 + k) for k in range(NKD)]
            steps += [(WOUT[:, k, :], wout_d[k * 128:(k + 1) * 128, :], D, 72 + k) for k in range(4)]
            for i, (dst_ap, src_ap, wdt, pcol) in enumerate(steps):
                sl = i % 2
                rdst = rWIN if i < NKD else rWOUT
                K.dma(SP, chSTGs[sl], STGs[sl][:, 0:wdt], src_ap, writes=[rSTGs[sl]])
                if sl == 0:
                    K.op(DVE, lambda dst_ap=dst_ap, sl=sl, wdt=wdt, pcol=pcol: nc.vector.tensor_scalar(
                        out=dst_ap, in0=STGs[sl][:, 0:wdt], scalar1=PV[:, pcol:pcol + 1], scalar2=None,
                        op0=ALU.mult), reads=[rSTGs[sl], rPV], writes=[rdst])
                else:
                    K.op(ACT, lambda dst_ap=dst_ap, sl=sl, wdt=wdt, pcol=pcol: nc.scalar.activation(
                        out=dst_ap, in_=STGs[sl][:, 0:wdt], func=AF.Copy, scale=PV[:, pcol:pcol + 1]),
                        reads=[rSTGs[sl], rPV], writes=[rdst])
            for k0 in range(4, NKD, 2):
                K.dma(POOL, chW, WOUT[:, k0:k0 + 2, :],
                      wout_d[k0 * 128:(k0 + 2) * 128, :].rearrange("(f p) d -> p f d", p=128), writes=[rWOUT])
            K.dma(POOL, chW, GW, gw_d.rearrange("p (a b) -> p a b", a=16), writes=[rGW])
            K.dma(POOL, chW, PWT, pw_d.rearrange("p (a b) -> p a b", a=4), writes=[rPWT])
            K.dma(SP, chGBm, GBm, gbc_d[:, D:2 * D], writes=[rGBm])
            K.barrier()

            ZM, rZM = [P[0], P[1]], [rP[0], rP[1]]
            ZH, rZH = P[2], rP[2]
            GPS = [([P[3], P[4]], [rP[3], rP[4]]), ([P[5], P[6]], [rP[5], rP[6]])]
            OL, rOL = P[5], rP[5]
            OP, rOP = P[6], rP[6]
            SSB, rSSB = P[7], rP[7]
            state = {"hw": 0, "zi": 0, "sq": 0, "b2": 0}

            def load_window(tok_s, t):
                slot = state["hw"] % 2
                state["hw"] += 1
                hw = HW[slot]
                c0 = tok_s + t * TT
                lo = 0 if t == 0 else 8
                hi = 0 if t == NT - 1 else 8
                K.dma(SP, chHW[slot], hw[:, :, 8 - lo:520 + hi],
                      ht_d[:, :, c0 - lo:c0 + TT + hi].rearrange("k p n -> p k n"), writes=[rHW[slot]], nbytes=1081344)
                if lo == 0:
                    K.op(POOL, lambda: nc.gpsimd.memset(hw[:, :, 0:8], 0.0), writes=[rHW[slot]], dur=0.3)
                if hi == 0:
                    K.op(POOL, lambda: nc.gpsimd.memset(hw[:, :, 520:528], 0.0), writes=[rHW[slot]], dur=0.3)
                return slot

            def inproj_chunk(slot, oc, halo):
                hw = HW[slot]
                zi = state["zi"]
                state["zi"] += 1
                zm, rzm = ZM[zi % 2], rZM[zi % 2]

                def fm():
                    ins = None
                    for k in range(NKD):
                        ins = nc.tensor.matmul(zm, lhsT=WIN[:, k, oc * 128:(oc + 1) * 128],
                                               rhs=hw[:, k, 8:520], start=(k == 0), stop=(k == NKD - 1))
                    return ins
                K.op(PE, fm, reads=[rHW[slot], rWIN], writes=[rzm], dur=1.76)
                if halo:
                    for side in range(2):
                        hc = 0 if side == 0 else 520

                        def fh(side=side, hc=hc):
                            ins = None
                            for k in range(NKD):
                                ins = nc.tensor.matmul(
                                    ZH[:, oc * 16 + side * 8:oc * 16 + side * 8 + 8],
                                    lhsT=WIN[:, k, oc * 128:(oc + 1) * 128], rhs=hw[:, k, hc:hc + 8],
                                    start=(k == 0), stop=(k == NKD - 1))
                            return ins
                        K.op(PE, fh, reads=[rHW[slot], rWIN], writes=[rZH], dur=0.56)
                return zm, rzm

            def evac_window(Z, rZc, c, oc, zm, rzm, main_eng):
                if main_eng is ACT:
                    K.op(ACT, lambda: nc.scalar.copy(out=Z[:, c, 8:520], in_=zm), reads=[rzm], writes=[rZc[c]])
                else:
                    K.op(DVE, lambda: nc.vector.tensor_copy(out=Z[:, c, 8:520], in_=zm), reads=[rzm],
                         writes=[rZc[c]])
                K.op(DVE, lambda: nc.vector.tensor_copy(out=Z[:, c, 0:8], in_=ZH[:, oc * 16:oc * 16 + 8]),
                     reads=[rZH], writes=[rZc[c]])
                K.op(DVE, lambda: nc.vector.tensor_copy(out=Z[:, c, 520:528], in_=ZH[:, oc * 16 + 8:oc * 16 + 16]),
                     reads=[rZH], writes=[rZc[c]], dur=0.17)

            FS0 = (ZX, rZX, XC, rXC, XCB, rXCB)
            FS1 = (ZP, rZP, GEL, rGEL, VB, rVB)

            def conv_chunk(c, fs):
                Zf, rZf, XCf, rXCf, XCBf, rXCBf = fs
                K.op(ACT, lambda: nc.scalar.activation(
                    out=XCf[:, c, :], in_=Zf[:, c, 6:6 + TT], func=AF.Identity, scale=PV[:, 24 + c:25 + c],
                    bias=PV[:, 40 + c:41 + c]),
                    reads=[rZf[c], rPV], writes=[rXCf[c]],
                    alts=[(POOL, lambda: nc.gpsimd.tensor_scalar(
                        out=XCf[:, c, :], in0=Zf[:, c, 6:6 + TT], scalar1=PV[:, 24 + c:25 + c],
                        scalar2=PV[:, 40 + c:41 + c], op0=ALU.mult, op1=ALU.add))])
                for kk in range(1, 4):
                    K.op(DVE, lambda kk=kk: nc.vector.scalar_tensor_tensor(
                        out=XCf[:, c, :], in0=Zf[:, c, 6 + kk:6 + kk + TT],
                        scalar=PV[:, 24 + kk * 4 + c:25 + kk * 4 + c], in1=XCf[:, c, :],
                        op0=ALU.mult, op1=ALU.add),
                        reads=[rZf[c], rPV, rXCf[c]], writes=[rXCf[c]])
                K.op(ACT, lambda: nc.scalar.copy(out=XCBf[:, c, :], in_=XCf[:, c, :]),
                     reads=[rXCf[c]], writes=[rXCBf[c]],
                     alts=[(DVE, lambda: nc.vector.tensor_copy(out=XCBf[:, c, :], in_=XCf[:, c, :])),
                           (POOL, lambda: nc.gpsimd.tensor_copy(out=XCBf[:, c, :], in_=XCf[:, c, :]))])

            def lru_stage(k, job, i):
                dr, c = job["dr"], job["c"]
                Zf, rZf, XCf, rXCf, XCBf, rXCBf = job.get("fs", FS0)
                T1, T2, T3 = TA[i]
                r1, r2, r3 = rTA[i]
                gp, rgp = GPS[i % 2]
                if k == 0:
                    ga = GW[:, dr * 8 + c, :]
                    gx = GW[:, dr * 8 + 4 + c, :]
                    K.op(PE, lambda: nc.tensor.matmul(gp[0], lhsT=ga, rhs=XCBf[:, c, :], start=True, stop=True),
                         reads=[rGW, rXCBf[c]], writes=[rgp[0]])
                    K.op(PE, lambda: nc.tensor.matmul(gp[1], lhsT=gx, rhs=XCBf[:, c, :], start=True, stop=True),
                         reads=[rGW, rXCBf[c]], writes=[rgp[1]])
                elif k == 1:
                    ba = PV[:, 44 + dr * 4 + c:45 + dr * 4 + c]
                    bx = PV[:, 52 + dr * 4 + c:53 + dr * 4 + c]
                    K.op(ACT, lambda: nc.scalar.activation(out=T1, in_=gp[0], func=AF.Sigmoid, bias=ba, scale=1.0),
                         reads=[rgp[0], rPV], writes=[r1], tbl="sig")
                    K.op(ACT, lambda: nc.scalar.activation(out=T2, in_=gp[1], func=AF.Sigmoid, bias=bx, scale=1.0),
                         reads=[rgp[1], rPV], writes=[r2], tbl="sig")
                elif k == 2:
                    nsp = PV[:, 80 + dr * 4 + c:81 + dr * 4 + c]
                    K.op(ACT, lambda: nc.scalar.activation(out=T3, in_=T1, func=AF.Exp, scale=nsp),
                         reads=[r1, rPV], writes=[r3], tbl="exp")
                    K.op(POOL, lambda: nc.gpsimd.tensor_tensor(out=T2, in0=T2, in1=XCf[:, c, :], op=ALU.mult),
                         reads=[r2, rXCf[c]], writes=[r2],
                         alts=[(DVE, lambda: nc.vector.tensor_tensor(out=T2, in0=T2, in1=XCf[:, c, :], op=ALU.mult))])
                elif k == 3:
                    K.op(ACT, lambda: nc.scalar.activation(out=T1, in_=T3, func=AF.Square),
                         reads=[r3, r1], writes=[r1],
                         alts=[(POOL, lambda: nc.gpsimd.tensor_tensor(out=T1, in0=T3, in1=T3, op=ALU.mult)),
                               (DVE, lambda: nc.vector.tensor_tensor(out=T1, in0=T3, in1=T3, op=ALU.mult))])
                elif k == 4:
                    K.op(ACT, lambda: nc.scalar.activation(out=T1, in_=T1, func=AF.Sqrt, bias=ONEC, scale=-1.0),
                         reads=[r1, rPV], writes=[r1], tbl="sqrt")
                elif k == 5:
                    K.op(DVE, lambda: nc.vector.tensor_tensor(out=T2, in0=T2, in1=T1, op=ALU.mult),
                         reads=[r1, r2], writes=[r2],
                         alts=[(POOL, lambda: nc.gpsimd.tensor_tensor(out=T2, in0=T2, in1=T1, op=ALU.mult))])
                elif k == 6:
                    out_ap, r_out = job["out"]
                    init_ap, r_init = job["init"]
                    if dr == 0:
                        K.op(DVE, lambda: nc.vector.tensor_tensor_scan(out=out_ap, data0=T3, data1=T2,
                                                                       initial=init_ap, op0=ALU.mult, op1=ALU.add),
                             reads=[r2, r3, r_init], writes=[r_out], cls="scan")
                    else:
                        K.op(DVE, lambda: nc.vector.tensor_tensor_scan(out=out_ap[:, ::-1], data0=T3[:, ::-1],
                                                                       data1=T2[:, ::-1], initial=init_ap,
                                                                       op0=ALU.mult, op1=ALU.add),
                             reads=[r2, r3, r_init], writes=[r_out], cls="scan")
                elif k == 7:
                    job["post"](job, i)

            def lru_A(jobs):
                for i, job in enumerate(jobs):
                    lru_stage(0, job, i)
                    lru_stage(1, job, i)
                for i, job in enumerate(jobs):
                    lru_stage(2, job, i)

            def lru_st(jobs, k0, k1):
                for k in range(k0, k1):
                    for i, job in enumerate(jobs):
                        lru_stage(k, job, i)

            def front(slot, fs=None):
                fs = fs or FS0
                for c in range(4):
                    zm, rzm = inproj_chunk(slot, c, True)
                    evac_window(fs[0], fs[1], c, c, zm, rzm, ACT)
                    conv_chunk(c, fs)

            K.begin_record()
            for s in range(nseq):
                tok_s = s * S
                rXCD = RL(NT)
                rHBD = [RL(4) for _ in range(NT)]
                for c in range(4):
                    K.op(POOL, lambda c=c: nc.gpsimd.memset(CBR[:, c:c + 1], 0.0), writes=[rCBR[c]], dur=0.3)
                b1_tiles = list(range(NT - 1, -1, -1))
                groups = [b1_tiles[i:i + 2] for i in range(0, len(b1_tiles), 2)]
                FSS = [FS0, FS1]
                for grp in groups:
                    slots = [load_window(tok_s, t) for t in grp]
                    for gi, t in enumerate(grp):
                        front(slots[gi], FSS[gi])
                        tok0 = tok_s + t * TT
                        K.dma(SP, chXCS[gi], xc_d[:, :, tok0:tok0 + TT].rearrange("c p n -> p c n"), FSS[gi][2],
                              reads=list(FSS[gi][3]), writes=[rXCD[t]], nbytes=1048576)
                    jobs = []
                    for gi, t in enumerate(grp):
                        for c in range(4):
                            i = gi * 4 + c
                            if gi == 0:
                                init = (CBR[:, c:c + 1], rCBR[c])
                            else:
                                init = (TA[c][0][:, 0:1], rTA[c][0])
                            last = (gi == len(grp) - 1)

                            def post_b1(job, i, t=t, last=last):
                                c = job["c"]
                                T1 = TA[i][0]
                                tok0 = tok_s + t * TT
                                if last:
                                    K.op(ACT, lambda: nc.scalar.copy(out=CBR[:, c:c + 1], in_=T1[:, 0:1]),
                                         reads=[rTA[i][0]], writes=[rCBR[c]], dur=0.31)
                                K.dma(SP, chHBS[i], hb_d[c, :, tok0:tok0 + TT], T1,
                                      reads=[rTA[i][0]], writes=[rHBD[t][c]], nbytes=262144)
                            jobs.append(dict(dr=1, c=c, init=init, out=(TA[i][0], rTA[i][0]), post=post_b1,
                                             fs=FSS[gi]))
                    lru_A(jobs)
                    lru_st(jobs, 3, 8)

                for c in range(4):
                    K.op(POOL, lambda c=c: nc.gpsimd.memset(CF[:, c:c + 1], 0.0), writes=[rCF[c]], dur=0.3)

                def load_b2(t):
                    sl = state["b2"] % 2
                    state["b2"] += 1
                    tok0 = tok_s + t * TT
                    K.dma(SP, chXCL[sl], XCS[sl], xc_d[:, :, tok0:tok0 + TT].rearrange("c p n -> p c n"),
                          reads=[rXCD[t]], writes=list(rXCS[sl]), nbytes=1048576)
                    K.dma(SP, chHBL[sl], HBS[sl], hb_d[:, :, tok0:tok0 + TT].rearrange("c p n -> p c n"),
                          reads=list(rHBD[t]), writes=list(rHBS[sl]), nbytes=1048576)
                    return sl
                slot = load_window(tok_s, 0)
                bsl = load_b2(0)
                pend = []
                for t in range(NT):
                    if t + 1 < NT:
                        nslot = load_window(tok_s, t + 1)
                        nbsl = load_b2(t + 1)
                    XCt, rXCt = XCS[bsl], rXCS[bsl]
                    HBt, rHBt = HBS[bsl], rHBS[bsl]
                    for c in range(4):
                        K.op(ACT, lambda c=c, XCt=XCt: nc.scalar.copy(out=XCB[:, c, :], in_=XCt[:, c, :]),
                             reads=[rXCt[c]], writes=[rXCB[c]],
                             alts=[(DVE, lambda c=c, XCt=XCt: nc.vector.tensor_copy(out=XCB[:, c, :], in_=XCt[:, c, :])),
                                   (POOL, lambda c=c, XCt=XCt: nc.gpsimd.tensor_copy(out=XCB[:, c, :],
                                                                                      in_=XCt[:, c, :]))])
                    for c in range(4):
                        zm, rzm = inproj_chunk(slot, 4 + c, False)
                        K.op(ACT, lambda c=c, zm=zm: nc.scalar.activation(out=GEL[:, c, :], in_=zm,
                                                                          func=AF.Gelu_apprx_tanh),
                             reads=[rzm], writes=[rGEL[c]], tbl="gelu")

                    def post_f(job, i, HBt=HBt, rHBt=rHBt):
                        c = job["c"]
                        K.op(ACT, lambda: nc.scalar.copy(out=CF[:, c:c + 1], in_=HF[:, c, TT - 1:TT]),
                             reads=[rHF[c]], writes=[rCF[c]], dur=0.31)
                        K.op(DVE, lambda: nc.vector.tensor_tensor(out=HF[:, c, :], in0=HF[:, c, :], in1=HBt[:, c, :],
                                                                  op=ALU.add),
                             reads=[rHF[c], rHBt[c]], writes=[rHF[c]],
                             alts=[(POOL, lambda: nc.gpsimd.tensor_tensor(out=HF[:, c, :], in0=HF[:, c, :],
                                                                          in1=HBt[:, c, :], op=ALU.add))])
                        K.op(POOL, lambda: nc.gpsimd.tensor_tensor(out=VB[:, c, :], in0=HF[:, c, :], in1=GEL[:, c, :],
                                                                   op=ALU.mult),
                             reads=[rHF[c], rGEL[c]], writes=[rVB[c]],
                             alts=[(DVE, lambda: nc.vector.tensor_tensor(out=VB[:, c, :], in0=HF[:, c, :],
                                                                         in1=GEL[:, c, :], op=ALU.mult))])
                        sqi = state["sq"] % 4
                        state["sq"] += 1
                        sq, rsq = SQ[sqi], rSQ[sqi]
                        K.op(ACT, lambda: nc.scalar.activation(out=sq, in_=VB[:, c, :], func=AF.Square),
                             reads=[rVB[c]], writes=[rsq],
                             alts=[(POOL, lambda: nc.gpsimd.tensor_tensor(out=sq, in0=VB[:, c, :], in1=VB[:, c, :],
                                                                          op=ALU.mult)),
                                   (DVE, lambda: nc.vector.tensor_tensor(out=sq, in0=VB[:, c, :], in1=VB[:, c, :],
                                                                         op=ALU.mult))])

                        def fss():
                            ins = None
                            for blk in range(4):
                                col = blk * 4 + c
                                ins = nc.tensor.matmul(SSB[:, col:col + 1], lhsT=sq[:, blk * 128:(blk + 1) * 128],
                                                       rhs=ONEB[:, 0:1], start=True, stop=True)
                            return ins
                        K.op(PE, fss, reads=[rsq, rONE], writes=[rSSB], dur=0.3)

                    fsb = (None, None, XCt, rXCt, XCB, rXCB)
                    jobs = []
                    for c in range(4):
                        jobs.append(dict(dr=0, c=c, init=(CF[:, c:c + 1], rCF[c]), out=(HF[:, c, :], rHF[c]),
                                         post=post_f, fs=fsb))
                    lru_A(jobs)
                    if pend:
                        pend.pop(0)()
                    for c in range(4):
                        zm, rzm = inproj_chunk(slot, 8 + c, True)
                        evac_window(ZP, rZP, c, 8 + c, zm, rzm, ACT)
                    lru_st(jobs, 3, 4)

                    for g in range(4):
                        w = WINS[g]
                        A, B = PA
                        rA, rB = rPA
                        K.op(POOL, lambda g=g: nc.gpsimd.tensor_tensor(out=A[:, 0:527], in0=ZP[:, g, 0:527],
                                                                       in1=ZP[:, g, 1:528], op=ALU.add),
                             reads=[rZP[g]], writes=[rA],
                             alts=[(DVE, lambda g=g: nc.vector.tensor_tensor(out=A[:, 0:527], in0=ZP[:, g, 0:527],
                                                                             in1=ZP[:, g, 1:528], op=ALU.add))])
                        cur, rcur, oth, roth = A, rA, B, rB
                        n = 527
                        step = 2
                        for lvl in range(g):
                            n2 = n - step
                            K.op(POOL, lambda cur=cur, oth=oth, n2=n2, step=step: nc.gpsimd.tensor_tensor(
                                out=oth[:, 0:n2], in0=cur[:, 0:n2], in1=cur[:, step:step + n2], op=ALU.add),
                                reads=[rcur], writes=[roth],
                                alts=[(DVE, lambda cur=cur, oth=oth, n2=n2, step=step: nc.vector.tensor_tensor(
                                    out=oth[:, 0:n2], in0=cur[:, 0:n2], in1=cur[:, step:step + n2], op=ALU.add))])
                            cur, rcur, oth, roth = oth, roth, cur, rcur
                            n = n2
                            step *= 2
                        off = 8 - w // 2
                        K.op(DVE, lambda g=g, cur=cur, off=off, w=w: nc.vector.scalar_tensor_tensor(
                            out=DD[:, g, :], in0=cur[:, off:off + TT], scalar=1.0 / w, in1=ZP[:, g, 8:8 + TT],
                            op0=ALU.mult, op1=ALU.subtract),
                            reads=[rcur, rZP[g]], writes=[rDD[g]])
                        fixes = []
                        if t == 0:
                            for tk in range(w // 2):
                                fixes.append((tk, tk + w // 2))
                        if t == NT - 1:
                            for m in range(1, w // 2):
                                fixes.append((TT - m, w // 2 + m))
                        for (col, cnt) in fixes:
                            K.op(DVE, lambda g=g, cur=cur, off=off, col=col, cnt=cnt: nc.vector.scalar_tensor_tensor(
                                out=DD[:, g, col:col + 1], in0=cur[:, off + col:off + col + 1], scalar=1.0 / cnt,
                                in1=ZP[:, g, 8 + col:9 + col], op0=ALU.mult, op1=ALU.subtract),
                                reads=[rcur, rZP[g], rDD[g]], writes=[rDD[g]], dur=0.17)
                        gp = ZM[g % 2]
                        rgp = rZM[g % 2]
                        K.op(PE, lambda g=g, gp=gp: nc.tensor.matmul(gp, lhsT=PWT[:, g, :], rhs=DD[:, g, :],
                                                                     start=True, stop=True),
                             reads=[rPWT, rDD[g]], writes=[rgp])
                        K.op(ACT, lambda g=g, gp=gp: nc.scalar.activation(out=PB[:, g, :], in_=gp, func=AF.Copy,
                                                                          scale=PV[:, 88 + g:89 + g]),
                             reads=[rgp, rPV], writes=[rPB[g]])
                        sqi = state["sq"] % 4
                        state["sq"] += 1
                        sq, rsq = SQ[sqi], rSQ[sqi]
                        K.op(ACT, lambda g=g, gp=gp, sq=sq: nc.scalar.activation(out=sq, in_=gp, func=AF.Square,
                                                                                 scale=PV[:, 68 + g:69 + g]),
                             reads=[rgp, rPV], writes=[rsq])

                        def fsp(g=g, sq=sq):
                            ins = None
                            for blk in range(4):
                                col = 16 + blk * 4 + g
                                ins = nc.tensor.matmul(SSB[:, col:col + 1], lhsT=sq[:, blk * 128:(blk + 1) * 128],
                                                       rhs=ONEB[:, 0:1], start=True, stop=True)
                            return ins
                        K.op(PE, fsp, reads=[rsq, rONE], writes=[rSSB], dur=0.3)
                    lru_st(jobs, 4, 8)
                    K.op(DVE, lambda: nc.vector.tensor_reduce(
                        out=ST3[:, 0:8], in_=SSB[:, 0:32].rearrange("p (a b) -> p a b", b=4), axis=AX.X, op=ALU.add),
                        reads=[rSSB], writes=[rst3[0]], dur=0.2)
                    K.op(ACT, lambda: nc.scalar.activation(out=ST3[:, 8:16], in_=ST3[:, 0:8], func=AF.Sqrt,
                                                           bias=EPSC, scale=1.0 / 512),
                         reads=[rst3[0]], writes=[rst3[1]], tbl="sqrt", dur=0.31)
                    K.op(DVE, lambda: nc.vector.reciprocal(out=ST3[:, 16:24], in_=ST3[:, 8:16]),
                         reads=[rst3[1]], writes=[rst3[2]], dur=0.2)
                    def s6(t=t):
                        for b in range(2):
                            K.dma(SP, chXR[b], XR[b], rows(src, tok_s + t * TT + b * 128), writes=[rXR[b]])
                        for b in range(4):
                            slot2 = b % 2
                            for hf in range(2):
                                def fol(b=b, hf=hf):
                                    ins = None
                                    for c in range(4):
                                        ins = nc.tensor.matmul(OL, lhsT=VB[:, c, b * 128:(b + 1) * 128],
                                                               rhs=WOUT[:, c, hf * 512:(hf + 1) * 512],
                                                               start=(c == 0), stop=(c == 3))
                                    return ins
                                K.op(PE, fol, reads=rVB + [rWOUT], writes=[rOL], dur=0.88)

                                def fop(b=b, hf=hf):
                                    ins = None
                                    for g in range(4):
                                        ins = nc.tensor.matmul(OP, lhsT=PB[:, g, b * 128:(b + 1) * 128],
                                                               rhs=WOUT[:, 4 + g, hf * 512:(hf + 1) * 512],
                                                               start=(g == 0), stop=(g == 3))
                                    return ins
                                K.op(PE, fop, reads=rPB + [rWOUT], writes=[rOP], dur=0.88)
                                oh = O[slot2][:, hf * 512:(hf + 1) * 512]
                                K.op(ACT, lambda oh=oh, b=b: nc.scalar.activation(out=oh, in_=OL, func=AF.Copy,
                                                                                  scale=ST3[:, 16 + b:17 + b]),
                                     reads=[rOL, rst3[2]], writes=[rO[slot2][hf]])
                                K.op(DVE, lambda oh=oh, b=b: nc.vector.scalar_tensor_tensor(
                                    out=oh, in0=OP, scalar=ST3[:, 20 + b:21 + b], in1=oh, op0=ALU.mult, op1=ALU.add),
                                    reads=[rOP, rst3[2], rO[slot2][hf]], writes=[rO[slot2][hf]])
                            K.op(ACT, lambda slot2=slot2, b=b: nc.scalar.activation(
                                out=ZX[:, 0, 0:512].bitcast(BF16), in_=O[slot2], func=AF.Square,
                                accum_out=ST3[:, 24 + b:25 + b]),
                                reads=rO[slot2], writes=[rst3[3 + b], rZX[0]], n=1024)
                            K.op(ACT, lambda b=b: nc.scalar.activation(out=ST3[:, 28 + b:29 + b], in_=ST3[:, 24 + b:25 + b],
                                                                       func=AF.Sqrt, bias=EPSC, scale=1.0 / D),
                                 reads=[rst3[3 + b]], writes=[rst3[7 + b]], tbl="sqrt", dur=0.31)
                            K.op(DVE, lambda b=b: nc.vector.reciprocal(out=ST3[:, 32 + b:33 + b], in_=ST3[:, 28 + b:29 + b]),
                                 reads=[rst3[7 + b]], writes=[rst3[11 + b]], dur=0.2)
                            K.op(DVE, lambda slot2=slot2, b=b: nc.vector.scalar_tensor_tensor(
                                out=O[slot2], in0=O[slot2], scalar=ST3[:, 32 + b:33 + b], in1=GBm,
                                op0=ALU.mult, op1=ALU.mult),
                                reads=rO[slot2] + [rst3[11 + b], rGBm], writes=rO[slot2], n=1024)
                            K.op(POOL, lambda slot2=slot2: nc.gpsimd.tensor_tensor(out=XR[slot2], in0=O[slot2],
                                                                                   in1=XR[slot2], op=ALU.add),
                                 reads=rO[slot2] + [rXR[slot2]], writes=[rXR[slot2]], n=1024,
                                 alts=[(DVE, lambda slot2=slot2: nc.vector.tensor_tensor(
                                     out=XR[slot2], in0=O[slot2], in1=XR[slot2], op=ALU.add))])
                            K.dma(SP, chST[slot2], rows(dst, tok_s + t * TT + b * 128), XR[slot2], reads=[rXR[slot2]])
                            if b + 2 < 4:
                                K.dma(SP, chXR[slot2], XR[slot2], rows(src, tok_s + t * TT + (b + 2) * 128),
                                      writes=[rXR[slot2]])
                    pend.append(s6)
                    if t + 1 < NT:
                        slot = nslot
                        bsl = nbsl
                while pend:
                    pend.pop(0)()
            K.flush(window=SCHED_WINDOW, verbose=True)
            K.barrier()

        ffn_phase(x_d, s1_d, w1a_d, w1b_d, 0, 0, "fa", True)
        mixer_phase(s1_d, s2_d)
        ffn_phase(s2_d, y_d, w2a_d, w2b_d, 16, 2, "fc", False)
        K.barrier()
    return nc


def host_prep(inputs, nseq_total):
    f = lambda a: np.ascontiguousarray(np.asarray(a, dtype=np.float32))
    pv = np.zeros((128, NPV), np.float32)

    def put(col0, vec, n):
        pv[:, col0:col0 + n] = f(vec).reshape(n, 128).T

    put(0, inputs["ffn1_pre_g"][0], 8)
    put(8, inputs["mix_pre_g"][0], 8)
    put(16, inputs["ffn2_pre_g"][0], 8)
    cw = f(inputs["conv_w"][0])
    for k in range(4):
        put(24 + 4 * k, cw[k], 4)
    put(40, inputs["conv_b"][0], 4)
    for dr in range(2):
        put(44 + 4 * dr, inputs["lru_b_a"][0][dr], 4)
        put(52 + 4 * dr, inputs["lru_b_x"][0][dr], 4)
        put(60 + 4 * dr, inputs["lru_lam"][0][dr], 4)
    put(68, inputs["pool_scale"][0], 4)
    put(72, inputs["lru_out_g"][0], 4)
    put(76, inputs["pool_out_g"][0], 4)
    gw = np.zeros((128, 16, 128), np.float32)
    wa = f(inputs["lru_w_a"][0])
    wx = f(inputs["lru_w_x"][0])
    for dr in range(2):
        for gi, wsrc in enumerate((wa, wx)):
            for c in range(4):
                for hh in range(2):
                    gw[hh * 64:(hh + 1) * 64, dr * 8 + gi * 4 + c, hh * 64:(hh + 1) * 64] = wsrc[dr, 2 * c + hh]
    pw = np.ascontiguousarray(f(inputs["pool_w"][0]).transpose(1, 0, 2))
    gbc = np.zeros((128, 3, D), np.float32)
    gbc[:, 0, :] = f(inputs["ffn1_post_g"][0])[None, :]
    gbc[:, 1, :] = f(inputs["mix_post_g"][0])[None, :]
    gbc[:, 2, :] = f(inputs["ffn2_post_g"][0])[None, :]
    shared = {
        "w1a": f(inputs["ffn1_w_in"][0]), "w1b": f(inputs["ffn1_w_out"][0]),
        "w2a": f(inputs["ffn2_w_in"][0]), "w2b": f(inputs["ffn2_w_out"][0]),
        "win": f(inputs["w_in"][0]), "wout": f(inputs["w_out"][0]),
        "gw": gw.reshape(128, 16 * 128), "pw": pw.reshape(128, 4 * 128),
        "pvec": pv, "gbc": gbc.reshape(128, 3 * D), "ident": np.eye(128, dtype=np.float32),
    }
    return shared


def run(inputs, n_cores=8):
    xp = np.asarray(inputs["x_prompt"], dtype=np.float32)
    xs = np.asarray(inputs["x_sample"], dtype=np.float32)
    S = xp.shape[1]
    assert xs.shape[1] == S
    allx = np.concatenate([xp, xs], axis=0)
    ntot = allx.shape[0]
    assert ntot % n_cores == 0
    nseq = ntot // n_cores
    shared = host_prep(inputs, ntot)
    nc = build(nseq, S)
    in_maps = []
    for c in range(n_cores):
        m = dict(shared)
        m["x"] = np.ascontiguousarray(allx[c * nseq:(c + 1) * nseq].reshape(nseq * S, D))
        in_maps.append(m)
    res = run_bass_kernel_spmd(nc, in_maps, core_ids=list(range(n_cores)))
    outs = [np.asarray(r["y"], dtype=np.float32).reshape(nseq, S, D) for r in res.results]
    ally = np.concatenate(outs, axis=0)
    return ally[:xp.shape[0]], ally[xp.shape[0]:]


def kernel(**inputs):
    yp, ys = run(inputs, 8)
    return (np.ascontiguousarray(yp), np.ascontiguousarray(ys))
```
